# Optimizing a Trainium2 kernel written in Bass

```python
import math
import jax, jax.numpy as jnp
from jax import lax
import numpy as np

D_MODEL = 4096
BATCH = 4
SEQ = 2048
DEPTH = 2
DEC_BATCH = 16
DEC_SEQ = 32
PAST_LEN = 1024

CHUNK = 64
D_INNER = 2 * D_MODEL
W_POOL = D_INNER // 4
POOL_WINDOWS = (2, 4, 8, 16)
N_POOL_GROUPS = len(POOL_WINDOWS)
POOL_GW = W_POOL // N_POOL_GROUPS
POOL_HIST = max(POOL_WINDOWS) - 1
W_SSD = D_INNER - W_POOL
SSD_HEAD_DIM = 64
SSD_HEADS = W_SSD // SSD_HEAD_DIM
SSD_GROUPS = 8
SSD_HPG = SSD_HEADS // SSD_GROUPS
SSD_STATE = 128
CONV_W = 4
CONV_DIM = W_SSD + 2 * SSD_GROUPS * SSD_STATE
IN_DIM = 2 * W_POOL + W_SSD + CONV_DIM + SSD_HEADS
ALPHA = (2 * DEPTH) ** 0.25
BETA = (8 * DEPTH) ** -0.25
LN_EPS = 1e-5
RMS_EPS = 1e-5

kernel_name = "pool_ssd_hymba_stream_step"


def layer_norm(x):
    x32 = x.astype(jnp.float32)
    mu = jnp.mean(x32, axis=-1, keepdims=True)
    var = jnp.mean(jnp.square(x32 - mu), axis=-1, keepdims=True)
    return (x32 - mu) * lax.rsqrt(var + LN_EPS)


def pool_mixer(u, prev, start_pos, w_pool, pool_scale):
    b, L, _ = u.shape
    full = jnp.concatenate([prev.astype(u.dtype), u], axis=1)
    new_prev = full[:, -POOL_HIST:]
    f32 = full.astype(jnp.float32).reshape(b, POOL_HIST + L, N_POOL_GROUPS, POOL_GW)
    cs = jnp.concatenate([jnp.zeros_like(f32[:, :1]), jnp.cumsum(f32, axis=1)], axis=1)
    end = cs[:, POOL_HIST + 1:]
    pos = start_pos + jnp.arange(L)
    means = []
    for g, w in enumerate(POOL_WINDOWS):
        s = end[:, :, g] - cs[:, POOL_HIST + 1 - w: POOL_HIST + 1 - w + L, g]
        cnt = jnp.minimum(pos + 1, w).astype(jnp.float32)
        means.append(s / cnt[None, :, None])
    mean = jnp.stack(means, axis=2)
    pooled = mean - f32[:, POOL_HIST:]
    out = jnp.einsum('blgc,gcd->blgd', pooled, w_pool.astype(jnp.float32)).reshape(b, L, W_POOL)
    return out * pool_scale.astype(jnp.float32), new_prev


def causal_conv(u, prev, w, bias):
    L = u.shape[1]
    full = jnp.concatenate([prev.astype(u.dtype), u], axis=1)
    new_prev = full[:, -(CONV_W - 1):]
    f32 = full.astype(jnp.float32)
    w32 = w.astype(jnp.float32)
    out = sum(f32[:, k:k + L] * w32[k] for k in range(CONV_W)) + bias.astype(jnp.float32)
    return jax.nn.silu(out), new_prev


def ssd_scan(xh, dt, A, Bm, Cm, h0):
    b, L = xh.shape[:2]
    nc = -(-L // CHUNK)
    pad = nc * CHUNK - L
    padt = lambda t: jnp.pad(t, [(0, 0), (0, pad)] + [(0, 0)] * (t.ndim - 2))
    a = padt(dt * A)
    xdt = padt(xh * dt[..., None])
    Bp, Cp = padt(Bm), padt(Cm)
    chunk = lambda t: t.reshape((b, nc, CHUNK) + t.shape[2:])
    a, xdt, Bc, Cc = chunk(a), chunk(xdt), chunk(Bp), chunk(Cp)
    a_cs = jnp.cumsum(a, axis=2)
    seg = a_cs[:, :, :, None] - a_cs[:, :, None, :]
    tril = jnp.tril(jnp.ones((CHUNK, CHUNK), dtype=bool))[None, None, :, :, None, None]
    Lmat = jnp.exp(jnp.where(tril, seg, -jnp.inf))
    CB = jnp.einsum('bclgn,bcsgn->bclsg', Cc, Bc)
    y_diag = jnp.einsum('bclsg,bclsgk,bcsgkp->bclgkp', CB, Lmat, xdt)
    decay = jnp.exp(a_cs[:, :, -1:] - a_cs)
    states = jnp.einsum('bcsgn,bcsgk,bcsgkp->bcgkpn', Bc, decay, xdt)
    chunk_decay = jnp.exp(a_cs[:, :, -1])

    def step(h, inp):
        dec, st = inp
        return dec[..., None, None] * h + st, h

    h_final, h_prev = lax.scan(step, h0.astype(jnp.float32),
                               (jnp.moveaxis(chunk_decay, 1, 0), jnp.moveaxis(states, 1, 0)))
    h_prev = jnp.moveaxis(h_prev, 0, 1)
    y_off = jnp.einsum('bclgn,bcgkpn,bclgk->bclgkp', Cc, h_prev, jnp.exp(a_cs))
    y = (y_diag + y_off).reshape((b, nc * CHUNK) + xh.shape[2:])[:, :L]
    return y, h_final


def trunk_layer(x, c, pool_prev, conv_prev, ssm_prev, start_pos,
                w_ada, b_ada, w_in, w_pool, pool_scale, conv_w, conv_b,
                dt_bias, a_log, d_skip, ssd_norm_w, w_out, ln_g, ln_b):
    b, L, _ = x.shape
    mod = (jnp.einsum('bd,de->be', jax.nn.silu(c), w_ada) + b_ada).astype(jnp.float32)
    shift, scale, gate = jnp.split(mod, 3, axis=-1)
    h = (layer_norm(x) * (1.0 + scale[:, None]) + shift[:, None]).astype(x.dtype)
    proj = jnp.einsum('bld,de->ble', h, w_in)
    u_pool, g_pool, z, xbc, dt_raw = jnp.split(
        proj, [W_POOL, 2 * W_POOL, 2 * W_POOL + W_SSD, 2 * W_POOL + W_SSD + CONV_DIM], axis=-1)
    pool_out, new_pool = pool_mixer(u_pool, pool_prev, start_pos, w_pool, pool_scale)
    pool_out = pool_out * jax.nn.silu(g_pool.astype(jnp.float32))
    xbc_c, new_conv = causal_conv(xbc, conv_prev, conv_w, conv_b)
    xs, Bm, Cm = jnp.split(xbc_c, [W_SSD, W_SSD + SSD_GROUPS * SSD_STATE], axis=-1)
    dt = jax.nn.softplus(dt_raw.astype(jnp.float32) + dt_bias.astype(jnp.float32))
    A = -jnp.exp(a_log.astype(jnp.float32))
    xh = xs.reshape(b, L, SSD_GROUPS, SSD_HPG, SSD_HEAD_DIM)
    y, new_ssm = ssd_scan(xh, dt.reshape(b, L, SSD_GROUPS, SSD_HPG), A.reshape(SSD_GROUPS, SSD_HPG),
                          Bm.reshape(b, L, SSD_GROUPS, SSD_STATE), Cm.reshape(b, L, SSD_GROUPS, SSD_STATE),
                          ssm_prev.reshape(b, SSD_GROUPS, SSD_HPG, SSD_HEAD_DIM, SSD_STATE))
    y = y + d_skip.astype(jnp.float32).reshape(SSD_GROUPS, SSD_HPG)[:, :, None] * xh
    y = y.reshape(b, L, W_SSD) * jax.nn.silu(z.astype(jnp.float32))
    yg = y.reshape(b, L, SSD_GROUPS, W_SSD // SSD_GROUPS)
    yg = yg * lax.rsqrt(jnp.mean(jnp.square(yg), axis=-1, keepdims=True) + RMS_EPS)
    ssd_out = yg.reshape(b, L, W_SSD) * ssd_norm_w.astype(jnp.float32)
    mixed = jnp.concatenate([pool_out, ssd_out], axis=-1).astype(x.dtype)
    o = jnp.einsum('ble,ed->bld', mixed, w_out).astype(jnp.float32)
    r = ALPHA * x.astype(jnp.float32) + gate[:, None] * o
    x_new = (layer_norm(r) * ln_g.astype(jnp.float32) + ln_b.astype(jnp.float32)).astype(x.dtype)
    return x_new, new_pool, new_conv, new_ssm.reshape(b, SSD_HEADS, SSD_HEAD_DIM, SSD_STATE)


def setup_inputs(seed: int = 0) -> dict:
    key = jax.random.key(seed)
    ks = jax.random.split(key, 24)
    f32 = jnp.float32
    nrm = lambda k, s, sc: jax.random.normal(k, s, f32) * sc
    dt0 = jnp.exp(jax.random.uniform(ks[15], (DEPTH, SSD_HEADS), f32) * (math.log(0.1) - math.log(0.001))
                  + math.log(0.001))
    return {
        "x_prompt": nrm(ks[0], (BATCH, SEQ, D_MODEL), 1.0),
        "x_sample": nrm(ks[1], (DEC_BATCH, DEC_SEQ, D_MODEL), 1.0),
        "state_pool": nrm(ks[2], (DEPTH, DEC_BATCH, POOL_HIST, W_POOL), 1.0),
        "state_conv": nrm(ks[3], (DEPTH, DEC_BATCH, CONV_W - 1, CONV_DIM), 1.0),
        "state_ssm": nrm(ks[4], (DEPTH, DEC_BATCH, SSD_HEADS, SSD_HEAD_DIM, SSD_STATE), 0.1),
        "c_prompt": nrm(ks[5], (BATCH, D_MODEL), 1.0),
        "c_sample": nrm(ks[6], (DEC_BATCH, D_MODEL), 1.0),
        "w_ada": nrm(ks[7], (DEPTH, D_MODEL, 3 * D_MODEL), 0.5 * D_MODEL ** -0.5),
        "b_ada": nrm(ks[8], (DEPTH, 3 * D_MODEL), 0.01),
        "w_in": nrm(ks[9], (DEPTH, D_MODEL, IN_DIM), D_MODEL ** -0.5),
        "w_pool": nrm(ks[10], (DEPTH, N_POOL_GROUPS, POOL_GW, POOL_GW), POOL_GW ** -0.5),
        "pool_scale": 1.0 + nrm(ks[11], (DEPTH, W_POOL), 0.02),
        "conv_w": nrm(ks[12], (DEPTH, CONV_W, CONV_DIM), 0.4),
        "conv_b": nrm(ks[13], (DEPTH, CONV_DIM), 0.01),
        "dt_bias": dt0 + jnp.log(-jnp.expm1(-dt0)),
        "a_log": jnp.log(jax.random.uniform(ks[16], (DEPTH, SSD_HEADS), f32, 1.0, 16.0)),
        "d_skip": 1.0 + nrm(ks[17], (DEPTH, SSD_HEADS), 0.1),
        "ssd_norm_w": 1.0 + nrm(ks[18], (DEPTH, W_SSD), 0.02),
        "w_out": nrm(ks[19], (DEPTH, D_INNER, D_MODEL), BETA * D_INNER ** -0.5),
        "ln_g": 1.0 + nrm(ks[20], (DEPTH, D_MODEL), 0.02),
        "ln_b": nrm(ks[21], (DEPTH, D_MODEL), 0.01),
    }


def reference(x_prompt, x_sample, state_pool, state_conv, state_ssm, c_prompt, c_sample,
              w_ada, b_ada, w_in, w_pool, pool_scale, conv_w, conv_b,
              dt_bias, a_log, d_skip, ssd_norm_w, w_out, ln_g, ln_b):
    nb = x_prompt.shape[0]
    yp, ys = x_prompt, x_sample
    pp, pc, psm, sp, sc, ssm = [], [], [], [], [], []
    for l in range(DEPTH):
        params = (w_ada[l], b_ada[l], w_in[l], w_pool[l], pool_scale[l], conv_w[l], conv_b[l],
                  dt_bias[l], a_log[l], d_skip[l], ssd_norm_w[l], w_out[l], ln_g[l], ln_b[l])
        yp, a1, a2, a3 = trunk_layer(
            yp, c_prompt,
            jnp.zeros((nb, POOL_HIST, W_POOL), x_prompt.dtype),
            jnp.zeros((nb, CONV_W - 1, CONV_DIM), x_prompt.dtype),
            jnp.zeros((nb, SSD_HEADS, SSD_HEAD_DIM, SSD_STATE), jnp.float32),
            0, *params)
        ys, b1, b2, b3 = trunk_layer(ys, c_sample, state_pool[l], state_conv[l], state_ssm[l],
                                     PAST_LEN, *params)
        pp.append(a1); pc.append(a2); psm.append(a3)
        sp.append(b1); sc.append(b2); ssm.append(b3)
    return (yp, ys, jnp.stack(pp), jnp.stack(pc), jnp.stack(psm),
            jnp.stack(sp), jnp.stack(sc), jnp.stack(ssm))
```

```python
import contextlib
import types
import numpy as np
import concourse.bass as bass
import concourse.mybir as mybir
from concourse.bass_utils import run_bass_kernel_spmd

F32 = mybir.dt.float32
BF16 = mybir.dt.bfloat16
AF = mybir.ActivationFunctionType
ALU = mybir.AluOpType

ENGS = ("pe", "act", "dve", "pool", "sp")
NDSEM = 8

D = 4096
NKT = 32
IN_DIM = 18528
W_POOL = 2048
W_SSD = 6144
CONV_DIM = 8192
ALPHA = 4.0 ** 0.25
LN_EPS = 1e-5
RMS_EPS = 1e-5
SEQ = 2048
TB = 512
NPB = SEQ // TB


class Op:
    __slots__ = ("eng", "fn", "dma", "deps", "signals", "count", "dsem", "dtarget", "idx", "qidx")

    def __init__(self, eng, fn, dma):
        self.eng = eng
        self.fn = fn
        self.dma = dma
        self.deps = set()
        self.signals = False
        self.count = 0
        self.dsem = None
        self.dtarget = 0
        self.idx = -1
        self.qidx = -1


def _freeze(fn):
    if not fn.__closure__:
        return fn
    cells = []
    for c in fn.__closure__:
        try:
            cells.append(types.CellType(c.cell_contents))
        except ValueError:
            cells.append(c)
    return types.FunctionType(fn.__code__, fn.__globals__, fn.__name__, fn.__defaults__, tuple(cells))


class Prog:
    def __init__(self, nc, same_engine_sync=True):
        self.nc = nc
        self.ops = []
        self.res = {}
        self.ndma = {e: 0 for e in ENGS}
        self.dma_ops = {e: [] for e in ENGS}
        self.same_engine_sync = same_engine_sync
        self.enabled = True
        self.maxops = 10 ** 9

    def add(self, eng, fn, reads=(), writes=(), dma=False, ph=True):
        if not self.enabled or len(self.ops) >= self.maxops:
            return None
        op = Op(eng, _freeze(fn), dma)
        op.idx = len(self.ops)
        reads = list(reads)
        if ph:
            reads.append("PH")
        for k in reads:
            r = self.res.setdefault(k, [None, []])
            if r[0] is not None:
                op.deps.add(r[0])
        for k in writes:
            r = self.res.setdefault(k, [None, []])
            if r[0] is not None:
                op.deps.add(r[0])
            op.deps.update(r[1])
        for k in reads:
            self.res[k][1].append(op.idx)
        for k in writes:
            self.res[k] = [op.idx, []]
        if dma:
            j = self.ndma[eng]
            op.qidx = j
            self.ndma[eng] += 1
            if j >= NDSEM:
                op.deps.add(self.dma_ops[eng][j - NDSEM].idx)
            self.dma_ops[eng].append(op)
        op.deps.discard(op.idx)
        self.ops.append(op)
        return op

    def emit(self):
        nc = self.nc
        ops = self.ops
        ses = self.same_engine_sync

        def skip(p, op):
            return (not p.dma) and p.eng == op.eng and (p.eng == "pe" or not ses) and not op.dma

        for op in ops:
            latest = {}
            keep = set()
            for d in op.deps:
                p = ops[d]
                if p.dma:
                    keep.add(d)
                elif latest.get(p.eng, -1) < d:
                    latest[p.eng] = d
            keep.update(latest.values())
            op.deps = keep
        for op in ops:
            for d in op.deps:
                p = ops[d]
                if p.dma or skip(p, op):
                    continue
                p.signals = True
        cnt = {e: 0 for e in ENGS}
        for op in ops:
            if not op.dma and op.signals:
                cnt[op.eng] += 1
                op.count = cnt[op.eng]
        with contextlib.ExitStack() as st:
            esem = {e: st.enter_context(nc.semaphore("s_" + e)) for e in ENGS}
            dsem = {}
            for e in ENGS:
                if self.ndma[e]:
                    dsem[e] = [st.enter_context(nc.semaphore("d_%s%d" % (e, i))) for i in range(NDSEM)]
            for e in ENGS:
                for op in self.dma_ops[e]:
                    op.dsem = dsem[e][op.qidx % NDSEM]
                    op.dtarget = 16 * (op.qidx // NDSEM + 1)
            block = st.enter_context(nc.Block())
            byeng = {e: [op for op in ops if op.eng == e] for e in ENGS}

            def run(e, eng):
                waited = {}
                for op in byeng[e]:
                    need = {}
                    for d in op.deps:
                        p = ops[d]
                        if p.dma:
                            key = ("d", p.eng, p.qidx % NDSEM)
                            sem, val = p.dsem, p.dtarget
                        else:
                            if skip(p, op):
                                continue
                            key = ("e", p.eng)
                            sem, val = esem[p.eng], p.count
                        if need.get(key, (None, 0))[1] < val:
                            need[key] = (sem, val)
                    for key, (sem, val) in need.items():
                        if waited.get(key, 0) >= val:
                            continue
                        waited[key] = val
                        eng.wait_ge(sem, val)
                    ins = op.fn(eng)
                    if op.dma:
                        ins.then_inc(op.dsem, 16)
                    elif op.signals:
                        ins.then_inc(esem[e], 1)
                if e == "sp":
                    for q in ENGS:
                        n = self.ndma[q]
                        for i in range(min(n, NDSEM)):
                            k = (n - i + NDSEM - 1) // NDSEM
                            eng.wait_ge(dsem[q][i], 16 * k)

            @block.tensor
            def _(eng):
                run("pe", eng)

            @block.scalar
            def _(eng):
                run("act", eng)

            @block.vector
            def _(eng):
                run("dve", eng)

            @block.gpsimd
            def _(eng):
                run("pool", eng)

            @block.sync
            def _(eng):
                run("sp", eng)


def build(cfg=None):
    cfg = cfg or {}
    LAYERS = cfg.get("layers", [0, 1])
    NPBLK = cfg.get("npblk", NPB)
    DO_SAMPLE = cfg.get("sample", True)

    nc = bass.Bass("TRN2", target_bir_lowering=False)

    FAKEW = cfg.get("fakew", False)

    def din(name, shape):
        if FAKEW and name in ("w_ada", "w_in", "w_out"):
            return None
        return nc.dram_tensor(name, list(shape), F32, kind="ExternalInput").ap()

    def dout(name, shape):
        return nc.dram_tensor(name, list(shape), F32, kind="ExternalOutput").ap()

    xp = din("xp", [SEQ, D])
    xs = din("xs", [64, D])
    cT = din("cT", [128, NKT, 3])
    spool = din("spool", [2, 2, 15, W_POOL])
    sconv = din("sconv", [2, 2, 3, CONV_DIM])
    sssm = din("sssm", [2, 2, W_SSD, 128])
    w_ada = din("w_ada", [2, D, 3 * D])
    badaT = din("badaT", [2, 128, 96])
    w_in = din("w_in", [2, D, IN_DIM])
    w_pool = din("w_pool", [2, 4, 512, 512])
    w_out = din("w_out", [2, 2 * D, D])
    pscT = din("pscT", [2, 128, 16])
    cwT = din("cwT", [2, 128, 64, 4])
    cbT = din("cbT", [2, 128, 64])
    nwT = din("nwT", [2, 128, 48])
    h96 = din("h96", [2, 3, 128, 96])
    lng_bc = din("lng_bc", [2, 128, D])
    lnb_bc = din("lnb_bc", [2, 128, D])
    c_ident = din("c_ident", [128, 128])
    c_ustrict = din("c_ustrict", [64, 64])
    c_tle = din("c_tle", [64, 64])
    c_invc = din("c_invc", [128, 4, 16])

    w_fake = nc.dram_tensor("w_fake", [8192, 256], F32, kind="ExternalInput").ap() if FAKEW else None
    if FAKEW:
        w_ada = w_in = w_out = [w_fake, w_fake]

    yp = dout("yp", [SEQ, D])
    ys = dout("ys", [64, D])
    npool_p = dout("npool_p", [2, 15, W_POOL])
    nconv_p = dout("nconv_p", [2, 3, CONV_DIM])
    nssm_p = dout("nssm_p", [2, W_SSD, 128])
    npool_s = dout("npool_s", [2, 2, 15, W_POOL])
    nconv_s = dout("nconv_s", [2, 2, 3, CONV_DIM])
    nssm_s = dout("nssm_s", [2, 2, W_SSD, 128])

    x1p = nc.dram_tensor("x1p", [SEQ, D], F32).ap()
    x1s = nc.dram_tensor("x1s", [64, D], F32).ap()
    oscr = nc.dram_tensor("oscr", [TB, D], F32).ap()
    stscr = nc.dram_tensor("stscr", [8, 128, 768], F32).ap()

    _uid = [0]

    def SBT(name, shape, dt=F32):
        _uid[0] += 1
        return nc.sbuf_tensor("%s_%d" % (name, _uid[0]), list(shape), dt)

    with contextlib.ExitStack() as st:
        def SB(name, shape, dt=F32):
            return st.enter_context(SBT(name, list(shape), dt))

        def PS(name, shape, dt=F32):
            return st.enter_context(nc.psum_tensor(name, list(shape), dt))

        wsl = [SB("wsl%d" % i, [128, 8192], BF16) for i in range(2)]
        ident = SB("ident", [128, 128])
        identb = SB("identb", [128, 128], BF16)
        ustrict = SB("ustrict", [64, 64])
        tle = SB("tle", [64, 64])
        ones = SB("ones", [128, 128])
        invc = SB("invc", [128, 4, 16])
        cTs = SB("cTs", [128, NKT, 3])
        scT = SB("scT", [128, NKT, 3], BF16)
        modT = SB("modT", [128, 96, 3])
        onep = SB("onep", [128, 32, 3])
        bada = SB("bada", [128, 96])
        psc = SB("psc", [128, 16])
        cw = SB("cw", [128, 64, 4])
        cb = SB("cb", [128, 64])
        nw = SB("nw", [128, 48])
        dtb = SB("dtb", [128, 96])
        Abc = SB("Abc", [128, 96])
        dsk = SB("dsk", [128, 96])
        uhist = SB("uhist", [128, 16, 2, 15])
        chist = SB("chist", [128, 64, 2, 3])
        scr1 = SB("scr1", [128, 8])

        pj = [PS("pj%d" % i, [128, 512]) for i in range(2)]
        misc = PS("misc", [128, 512])
        tpx = PS("tpx", [128, 1024], BF16)
        bigA = PS("bigA", [128, 2, 512])
        bigB = PS("bigB", [128, 2, 512])

        P = Prog(nc)
        P.maxops = cfg.get("maxops", 10 ** 9)
        state = {"slab": 0, "pj": 0, "ev": 0}

        STAGE = cfg.get("stage", 99)

        def stage(k):
            if k > STAGE:
                P.enabled = False

        def barrier():
            P.add("dve", lambda e: e.memset(scr1[:, 0:1], 0.0), writes=["PH"], ph=False)

        def load_slab_in(wsrc, pieces, nkt=NKT, width=256):
            b = state["slab"] % 2
            state["slab"] += 1
            view = wsl[b][:, 0:nkt * width].rearrange("p (k c) -> p k c", k=nkt)
            if FAKEW:
                wsrc = w_fake[0:nkt * 128, :]
                pieces = [(0, ncol, dc) for (c0, ncol, dc) in pieces]
            src = wsrc.rearrange("(kt p) c -> p kt c", p=128)
            for (c0, ncol, dc) in pieces:
                for q in range(nkt // 8):
                    P.add("pool", lambda e, c0=c0, ncol=ncol, dc=dc, q=q, view=view, src=src: e.dma_start(
                        out=view[:, q * 8:(q + 1) * 8, dc:dc + ncol], in_=src[:, q * 8:(q + 1) * 8, c0:c0 + ncol]),
                        writes=["wsl%d" % b], dma=True, ph=False)
            return b, view

        def next_pj():
            i = state["pj"] % 2
            state["pj"] += 1
            return i

        def setup():
            P.add("sp", lambda e: e.dma_start(out=ident[:], in_=c_ident[:, :]), writes=["ident"], dma=True)
            P.add("sp", lambda e: e.dma_start(out=ustrict[:], in_=c_ustrict[:, :]), writes=["consts"], dma=True)
            P.add("sp", lambda e: e.dma_start(out=tle[:], in_=c_tle[:, :]), writes=["consts"], dma=True)
            P.add("sp", lambda e: e.dma_start(out=invc[:], in_=c_invc[:, :, :]), writes=["consts"], dma=True)
            P.add("sp", lambda e: e.dma_start(out=cTs[:], in_=cT[:, :, :]), writes=["cTs"], dma=True)
            P.add("dve", lambda e: e.tensor_copy(identb[:], ident[:]), reads=["ident"], writes=["identb"])
            P.add("dve", lambda e: e.memset(ones[:], 1.0), writes=["ones"])
            P.add("act", lambda e: e.activation(out=scT[:], in_=cTs[:], func=AF.Silu), reads=["cTs"], writes=["scT"])

        def layer_setup(l):
            stage(1)
            barrier()
            P.add("sp", lambda e: e.dma_start(out=bada[:], in_=badaT[l]), writes=["bada"], dma=True)
            P.add("sp", lambda e: e.dma_start(out=psc[:], in_=pscT[l]), writes=["lp"], dma=True)
            P.add("sp", lambda e: e.dma_start(out=cw[:], in_=cwT[l]), writes=["lp"], dma=True)
            P.add("sp", lambda e: e.dma_start(out=cb[:], in_=cbT[l]), writes=["lp"], dma=True)
            P.add("sp", lambda e: e.dma_start(out=nw[:], in_=nwT[l]), writes=["lp"], dma=True)
            P.add("sp", lambda e: e.dma_start(out=dtb[:], in_=h96[l, 0]), writes=["lp"], dma=True)
            P.add("sp", lambda e: e.dma_start(out=Abc[:], in_=h96[l, 1]), writes=["Abc"], dma=True)
            P.add("sp", lambda e: e.dma_start(out=dsk[:], in_=h96[l, 2]), writes=["lp"], dma=True)
            P.add("act", lambda e: e.activation(out=Abc[:], in_=Abc[:], func=AF.Exp), reads=["Abc"], writes=["Abc"])
            P.add("dve", lambda e: e.tensor_scalar(Abc[:], Abc[:], -1.0, None, ALU.mult), reads=["Abc"], writes=["Abc"])
            P.add("dve", lambda e: e.memset(uhist[:], 0.0), writes=["uhist"])
            P.add("dve", lambda e: e.memset(chist[:], 0.0), writes=["chist"])
            for s in range(48):
                b, view = load_slab_in(w_ada[l], [(s * 256, 256, 0)])
                for cbi in range(2):
                    cbk = 2 * s + cbi
                    for kt in range(NKT):
                        P.add("pe", lambda e, view=view, cbi=cbi, cbk=cbk, kt=kt: e.matmul(
                            pj[0][:, cbk * 3:cbk * 3 + 3], view[:, kt, cbi * 128:(cbi + 1) * 128], scT[:, kt, :],
                            start=(kt == 0), stop=(kt == NKT - 1)),
                            reads=["wsl%d" % b, "scT"], writes=["pj0"])
            P.add("dve", lambda e: e.tensor_tensor(
                modT[:], pj[0][:, 0:288].rearrange("p (c r) -> p c r", r=3),
                bada[:].unsqueeze(2).broadcast_to([128, 96, 3]), ALU.add),
                reads=["pj0", "bada"], writes=["modT"])
            P.add("dve", lambda e: e.tensor_scalar(onep[:], modT[:, 32:64, :], 1.0, None, ALU.add),
                  reads=["modT"], writes=["onep"])

        def process_block(l, blk):
            kind = blk["kind"]
            NT = blk["NT"]
            nseg = blk["nseg"]
            Ls = blk["Ls"]
            xsrc, xdst = blk["src"], blk["dst"]
            ntt = (NT + 127) // 128
            npt = min(NT, 128)
            if kind == "p":
                chunks = [(0, 64 * c, 64, 0) for c in range(NT // 64)]
                segr = [0]
            else:
                chunks = [(s, 32 * s, 32, 1 + s) for s in range(2)]
                segr = [1, 2]
            first = blk.get("first", False)
            last = blk.get("last", False)

            barrier()
            with contextlib.ExitStack() as bst:
                def BB(name, shape, dt=F32):
                    return bst.enter_context(SBT(name, list(shape), dt))

                mixedT = BB("mixedT", [128, 64, TB], BF16)
                hT = BB("hT", [128, NKT, TB], BF16)

                stage(2)
                with contextlib.ExitStack() as ast:
                    X = ast.enter_context(SBT("XA", [128, D], F32))
                    XN = ast.enter_context(SBT("XNA", [128, D], BF16))
                    stt = ast.enter_context(SBT("sttA", [128, 8, 6], F32))
                    mv = ast.enter_context(SBT("mvA", [128, 2], F32))
                    rs = ast.enter_context(SBT("rsA", [128, 1], F32))
                    nb = ast.enter_context(SBT("nbA", [128, 1], F32))
                    for tt in range(ntt):
                        r0 = tt * 128
                        P.add("sp", lambda e, r0=r0: e.dma_start(out=X[0:npt, :], in_=xsrc[r0:r0 + npt, :]),
                              reads=[blk["srck"]], writes=["XA"], dma=True)
                        for c in range(8):
                            P.add("dve", lambda e, c=c: e.bn_stats(stt[0:npt, c, :], X[0:npt, c * 512:(c + 1) * 512]),
                                  reads=["XA"], writes=["sttA"])
                        P.add("dve", lambda e: e.bn_aggr(mv[0:npt, :], stt[0:npt, :, :]), reads=["sttA"], writes=["mvA"])
                        P.add("act", lambda e: e.activation(out=rs[0:npt, :], in_=mv[0:npt, 1:2], func=AF.Ln, bias=LN_EPS),
                              reads=["mvA"], writes=["rsA"])
                        P.add("act", lambda e: e.activation(out=rs[0:npt, :], in_=rs[0:npt, :], func=AF.Exp, scale=-0.5),
                              reads=["rsA"], writes=["rsA"])
                        P.add("dve", lambda e: e.scalar_tensor_tensor(nb[0:npt, :], mv[0:npt, 0:1], -1.0, rs[0:npt, :],
                                                                      ALU.mult, ALU.mult),
                              reads=["mvA", "rsA"], writes=["nbA"])
                        P.add("act", lambda e: e.activation(out=XN[0:npt, :], in_=X[0:npt, :], func=AF.Identity,
                                                            bias=nb[0:npt, :], scale=rs[0:npt, :]),
                              reads=["XA", "rsA", "nbA"], writes=["XNA"])
                        for q in range(4):
                            for i in range(8):
                                kt = q * 8 + i
                                P.add("pe", lambda e, i=i, kt=kt: e.transpose(
                                    tpx[:, i * 128:i * 128 + npt], XN[0:npt, kt * 128:(kt + 1) * 128], identb[0:npt, 0:npt]),
                                    reads=["XNA", "identb"], writes=["tpx"])
                            for i in range(8):
                                kt = q * 8 + i
                                for si in range(nseg):
                                    r = segr[si]
                                    if kind == "p":
                                        c0, c1 = 0, npt
                                    else:
                                        c0, c1 = si * Ls, (si + 1) * Ls
                                    eng = "dve" if (i % 2 == 0) else "pool_never"
                                    P.add("dve", lambda e, i=i, kt=kt, r=r, c0=c0, c1=c1, r0=r0: e.tensor_scalar(
                                        hT[:, kt, r0 + c0:r0 + c1], tpx[:, i * 128 + c0:i * 128 + c1],
                                        onep[:, kt, r:r + 1], modT[:, kt, r:r + 1], ALU.mult, ALU.add),
                                        reads=["tpx", "onep", "modT"], writes=["hT"])
                barrier()

                def proj_cb(view, b, cbi, evac):
                    i = next_pj()
                    for kt in range(NKT):
                        P.add("pe", lambda e, kt=kt, i=i: e.matmul(
                            pj[i][:, 0:NT], view[:, kt, cbi * 128:(cbi + 1) * 128], hT[:, kt, 0:NT],
                            start=(kt == 0), stop=(kt == NKT - 1)),
                            reads=["wsl%d" % b, "hT"], writes=["pj%d" % i])
                    evac(pj[i], "pj%d" % i)

                stage(3)
                with contextlib.ExitStack() as cst:
                    def CB_(name, shape, dt=F32):
                        return cst.enter_context(SBT(name, list(shape), dt))
                    dtall = CB_("dtall", [64, 8, 96])
                    aall = CB_("aall", [64, 8, 96])
                    wdt_cm = SBT("wdt", [128, NKT, 96], BF16)
                    wdt = wdt_cm.__enter__()
                    srcdt = (w_fake[0:D, :] if FAKEW else w_in[l]).rearrange("(kt p) c -> p kt c", p=128)
                    dtc0 = 0 if FAKEW else IN_DIM - 96
                    for q in range(4):
                        P.add("pool", lambda e, q=q: e.dma_start(out=wdt[:, q * 8:(q + 1) * 8, :],
                                                                 in_=srcdt[:, q * 8:(q + 1) * 8, dtc0:dtc0 + 96]),
                              writes=["wdt"], dma=True)
                    for ci, (sg, t0, Lc, r) in enumerate(chunks):
                        for kt in range(NKT):
                            P.add("pe", lambda e, kt=kt, t0=t0, Lc=Lc: e.matmul(
                                misc[0:Lc, 0:96], hT[:, kt, t0:t0 + Lc], wdt[:, kt, :],
                                start=(kt == 0), stop=(kt == NKT - 1)),
                                reads=["hT", "wdt"], writes=["misc"])
                        P.add("dve", lambda e, ci=ci, Lc=Lc: e.tensor_tensor(
                            dtall[0:Lc, ci, :], misc[0:Lc, 0:96], dtb[0:Lc, :], ALU.add),
                            reads=["misc", "lp"], writes=["dtall"])
                        P.add("act", lambda e, ci=ci, Lc=Lc: e.activation(out=dtall[0:Lc, ci, :], in_=dtall[0:Lc, ci, :], func=AF.Exp),
                              reads=["dtall"], writes=["dtall"])
                        P.add("act", lambda e, ci=ci, Lc=Lc: e.activation(out=dtall[0:Lc, ci, :], in_=dtall[0:Lc, ci, :], func=AF.Ln, bias=1.0),
                              reads=["dtall"], writes=["dtall"])
                        P.add("dve", lambda e, ci=ci, Lc=Lc: e.tensor_tensor(
                            aall[0:Lc, ci, :], dtall[0:Lc, ci, :], Abc[0:Lc, :], ALU.mult),
                            reads=["dtall", "Abc"], writes=["aall"])

                    barrier()
                    wdt_cm.__exit__(None, None, None)

                    stage(4)
                    with contextlib.ExitStack() as pst:
                        def PB(name, shape, dt=F32):
                            return pst.enter_context(SBT(name, list(shape), dt))
                        SW = nseg * (15 + Ls)
                        uf = PB("uf", [128, SW])
                        ta = PB("pta", [128, SW])
                        tb = PB("ptb", [128, SW])
                        tcn = PB("ptc", [128, 16])
                        wpl = PB("wpl", [128, 4, 512], BF16)
                        pooled = PB("pooled", [128, 4, TB], BF16)
                        gate = PB("gate", [128, 4, TB], BF16)
                        for p in range(4):
                            w = 2 ** (p + 1)
                            for half in range(2):
                                b, view = load_slab_in(w_in[l], [(p * 512 + half * 256, 256, 0)])
                                for cbi in range(2):
                                    j = half * 2 + cbi
                                    ct = p * 4 + j

                                    def evac_u(ps, psk, j=j, ct=ct, w=w, p=p):
                                        for si in range(nseg):
                                            base = si * (15 + Ls)
                                            P.add("act", lambda e, si=si, base=base: e.copy(
                                                uf[:, base + 15:base + 15 + Ls], ps[:, si * Ls:(si + 1) * Ls]),
                                                reads=[psk], writes=["uf"])
                                            P.add("dve", lambda e, si=si, base=base: e.tensor_copy(
                                                uf[:, base:base + 15], uhist[:, ct, si, :]),
                                                reads=["uhist"], writes=["uf"])
                                            P.add("dve", lambda e, si=si, base=base: e.tensor_copy(
                                                uhist[:, ct, si, :], uf[:, base + Ls:base + Ls + 15]),
                                                reads=["uf"], writes=["uhist"])
                                            src, srck = uf, "uf"
                                            tmps = [(ta, "pta"), (tb, "ptb")]
                                            for k in range(p + 1):
                                                dd = 2 ** k
                                                c0 = base + 2 ** (k + 1) - 1
                                                c1 = base + 15 + Ls
                                                dst, dstk = tmps[k % 2]
                                                P.add("dve", lambda e, src=src, dst=dst, c0=c0, c1=c1, dd=dd: e.tensor_tensor(
                                                    dst[:, c0:c1], src[:, c0:c1], src[:, c0 - dd:c1 - dd], ALU.add),
                                                    reads=[srck], writes=[dstk])
                                                src, srck = dst, dstk
                                            P.add("dve", lambda e, src=src, base=base, si=si: e.scalar_tensor_tensor(
                                                pooled[:, j, si * Ls:(si + 1) * Ls], src[:, base + 15:base + 15 + Ls], 1.0 / w,
                                                uf[:, base + 15:base + 15 + Ls], ALU.mult, ALU.subtract),
                                                reads=[srck, "uf"], writes=["pooled"])
                                            if kind == "p" and first:
                                                P.add("dve", lambda e, src=src: e.tensor_tensor(
                                                    tcn[:, :], src[:, 15:31], invc[:, p, :], ALU.mult),
                                                    reads=[srck, "consts"], writes=["ptc"])
                                                P.add("dve", lambda e: e.tensor_tensor(
                                                    pooled[:, j, 0:16], tcn[:, :], uf[:, 15:31], ALU.subtract),
                                                    reads=["ptc", "uf"], writes=["pooled"])
                                    proj_cb(view, b, cbi, evac_u)
                            for half in range(2):
                                b, view = load_slab_in(w_in[l], [(W_POOL + p * 512 + half * 256, 256, 0)])
                                for cbi in range(2):
                                    j = half * 2 + cbi

                                    def evac_g(ps, psk, j=j):
                                        P.add("act", lambda e: e.activation(out=gate[:, j, 0:NT], in_=ps[:, 0:NT], func=AF.Silu),
                                              reads=[psk], writes=["gate"])
                                    proj_cb(view, b, cbi, evac_g)
                            srcp = w_pool[l, p].rearrange("(kt q) d -> q kt d", q=128)
                            P.add("pool", lambda e, srcp=srcp: e.dma_start(out=wpl[:], in_=srcp), writes=["wpl"], dma=True)
                            for db in range(4):
                                i = next_pj()
                                for kt in range(4):
                                    P.add("pe", lambda e, i=i, kt=kt, db=db: e.matmul(
                                        pj[i][:, 0:NT], wpl[:, kt, db * 128:(db + 1) * 128], pooled[:, kt, 0:NT],
                                        start=(kt == 0), stop=(kt == 3)),
                                        reads=["wpl", "pooled"], writes=["pj%d" % i])
                                P.add("dve", lambda e, i=i, db=db, p=p: e.scalar_tensor_tensor(
                                    mixedT[:, p * 4 + db, 0:NT], pj[i][:, 0:NT], psc[:, p * 4 + db:p * 4 + db + 1],
                                    gate[:, db, 0:NT], ALU.mult, ALU.mult),
                                    reads=["pj%d" % i, "gate", "lp"], writes=["mx%d" % (p * 4 + db)])
                    barrier()

                    stage(5)
                    with contextlib.ExitStack() as sst:
                        def SS(name, shape, dt=F32):
                            return sst.enter_context(SBT(name, list(shape), dt))
                        CW = nseg * (3 + Ls)
                        xc = SS("xc", [128, 6, TB], BF16)
                        Bf = SS("Bf", [128, TB], BF16)
                        Cf = SS("Cf", [128, TB], BF16)
                        xpre = SS("xpre", [128, CW])
                        acc = SS("acc", [128, TB])
                        xs_tm = SS("xs_tm", [64, 768], BF16)
                        xdt = SS("xdt", [64, 768], BF16)
                        xdd = SS("xdd", [64, 768], BF16)
                        B_tm = SS("B_tm", [64, 128], BF16)
                        CBm = SS("CBm", [64, 64])
                        arhs = SS("arhs", [64, 768])
                        E = SS("E", [64, 768], BF16)
                        MT = SS("MT", [64, 768], BF16)
                        y1 = SS("y1", [64, 768])
                        yg = SS("yg", [128, 6, 64])
                        sq = SS("sq", [128, 6, 64])
                        S = SS("S", [128, 768])
                        S_bf = SS("S_bf", [128, 768], BF16)
                        ea = SS("ea", [64, 12])
                        cd = SS("cd", [128, 12])
                        rstd = SS("rstd", [128, 64])
                        stg = SS("stg", [128, 6, 128])

                        def conv_tile(ps, psk, ct, out_ap_fn, outk):
                            for si in range(nseg):
                                base = si * (3 + Ls)
                                P.add("act", lambda e, si=si, base=base: e.copy(
                                    xpre[:, base + 3:base + 3 + Ls], ps[:, si * Ls:(si + 1) * Ls]),
                                    reads=[psk], writes=["xpre"])
                                P.add("dve", lambda e, si=si, base=base: e.tensor_copy(
                                    xpre[:, base:base + 3], chist[:, ct, si, :]), reads=["chist"], writes=["xpre"])
                                P.add("dve", lambda e, si=si, base=base: e.tensor_copy(
                                    chist[:, ct, si, :], xpre[:, base + Ls:base + Ls + 3]), reads=["xpre"], writes=["chist"])
                                a0 = si * Ls
                                P.add("dve", lambda e, base=base, a0=a0: e.tensor_scalar(
                                    acc[:, a0:a0 + Ls], xpre[:, base:base + Ls], cw[:, ct, 0:1], cb[:, ct:ct + 1], ALU.mult, ALU.add),
                                    reads=["xpre", "lp"], writes=["acc"])
                                for k in range(1, 4):
                                    P.add("dve", lambda e, base=base, a0=a0, k=k: e.scalar_tensor_tensor(
                                        acc[:, a0:a0 + Ls], xpre[:, base + k:base + k + Ls], cw[:, ct, k:k + 1], acc[:, a0:a0 + Ls],
                                        ALU.mult, ALU.add),
                                        reads=["xpre", "lp", "acc"], writes=["acc"])
                            P.add("act", lambda e: e.activation(out=out_ap_fn(), in_=acc[:, 0:NT], func=AF.Silu),
                                  reads=["acc"], writes=[outk])

                        for g in range(8):
                            for s3 in range(3):
                                b, view = load_slab_in(w_in[l], [(2 * W_POOL + 768 * g + 256 * s3, 256, 0)])
                                for cbi in range(2):
                                    j = s3 * 2 + cbi

                                    def evac_z(ps, psk, j=j, g=g):
                                        P.add("act", lambda e: e.activation(out=mixedT[:, 16 + 6 * g + j, 0:NT], in_=ps[:, 0:NT], func=AF.Silu),
                                              reads=[psk], writes=["zs"])
                                    proj_cb(view, b, cbi, evac_z)
                            XB0 = 2 * W_POOL + W_SSD
                            for s3 in range(3):
                                b, view = load_slab_in(w_in[l], [(XB0 + 768 * g + 256 * s3, 256, 0)])
                                for cbi in range(2):
                                    j = s3 * 2 + cbi

                                    def evac_x(ps, psk, j=j):
                                        conv_tile(ps, psk, 6 * g + j, lambda: xc[:, j, 0:NT], "xc")
                                    proj_cb(view, b, cbi, evac_x)
                            b, view = load_slab_in(w_in[l], [(XB0 + W_SSD + 128 * g, 128, 0),
                                                             (XB0 + W_SSD + 1024 + 128 * g, 128, 128)])
                            proj_cb(view, b, 0, lambda ps, psk: conv_tile(ps, psk, 48 + g, lambda: Bf[:, 0:NT], "Bf"))
                            proj_cb(view, b, 1, lambda ps, psk: conv_tile(ps, psk, 56 + g, lambda: Cf[:, 0:NT], "Cf"))

                            g12 = g * 12
                            if kind == "p":
                                if first:
                                    P.add("dve", lambda e: e.memset(S[:], 0.0), writes=["S"])
                                else:
                                    P.add("sp", lambda e, g=g: e.dma_start(out=S[:], in_=stscr[g]),
                                          reads=["stscr%d" % g], writes=["S"], dma=True)
                                P.add("act", lambda e: e.copy(S_bf[:], S[:]), reads=["S"], writes=["S_bf"])

                            for ci, (sg, t0, Lc, r) in enumerate(chunks):
                                cols = slice(t0, t0 + Lc)
                                if kind == "s":
                                    srcS = sssm[l, sg, 768 * g:768 * (g + 1), :].rearrange("(j q) n -> q j n", q=128)
                                    P.add("sp", lambda e, srcS=srcS: e.dma_start(out=stg[:], in_=srcS), writes=["stg"], dma=True)
                                    for j in range(6):
                                        P.add("pe", lambda e, j=j: e.transpose(
                                            bigB[:, j // 3, (j % 3) * 128:(j % 3 + 1) * 128], stg[:, j, :], ident[:, :]),
                                            reads=["stg", "ident"], writes=["bigB"])
                                    P.add("act", lambda e: e.copy(
                                        S[:].rearrange("n (a b) -> n a b", a=2), bigB[:, :, 0:384]),
                                        reads=["bigB"], writes=["S"])
                                    P.add("act", lambda e: e.copy(S_bf[:], S[:]), reads=["S"], writes=["S_bf"])
                                stage(5.1)
                                for j in range(6):
                                    P.add("pe", lambda e, j=j, cols=cols, Lc=Lc: e.transpose(
                                        tpx[0:Lc, j * 128:(j + 1) * 128], xc[:, j, cols], identb[:, :]),
                                        reads=["xc", "identb"], writes=["tpx"])
                                P.add("pe", lambda e, cols=cols, Lc=Lc: e.transpose(
                                    tpx[0:Lc, 768:896], Bf[:, cols], identb[:, :]),
                                    reads=["Bf", "identb"], writes=["tpx"])
                                P.add("act", lambda e, Lc=Lc: e.copy(xs_tm[0:Lc, :], tpx[0:Lc, 0:768]), reads=["tpx"], writes=["xs_tm"])
                                P.add("act", lambda e, Lc=Lc: e.copy(B_tm[0:Lc, :], tpx[0:Lc, 768:896]), reads=["tpx"], writes=["B_tm"])
                                P.add("dve", lambda e, Lc=Lc, ci=ci: e.tensor_tensor(
                                    xdt[0:Lc, :].rearrange("l (h p) -> l h p", h=12),
                                    xs_tm[0:Lc, :].rearrange("l (h p) -> l h p", h=12),
                                    dtall[0:Lc, ci, g12:g12 + 12].unsqueeze(2).broadcast_to([Lc, 12, 64]), ALU.mult),
                                    reads=["xs_tm", "dtall"], writes=["xdt"])
                                stage(5.2)
                                P.add("pe", lambda e, cols=cols, Lc=Lc: e.matmul(
                                    misc[0:Lc, 0:Lc], Bf[:, cols], Cf[:, cols], start=True, stop=True),
                                    reads=["Bf", "Cf"], writes=["misc"])
                                P.add("dve", lambda e, Lc=Lc: e.tensor_tensor(
                                    CBm[0:Lc, 0:Lc], misc[0:Lc, 0:Lc], tle[0:Lc, 0:Lc], ALU.mult),
                                    reads=["misc", "consts"], writes=["CBm"])
                                stage(5.3)
                                P.add("dve", lambda e, Lc=Lc, ci=ci: e.tensor_tensor(
                                    arhs[0:Lc, 0:12 * Lc].rearrange("t (h l) -> t h l", h=12),
                                    aall[0:Lc, ci, g12:g12 + 12].unsqueeze(2).broadcast_to([Lc, 12, Lc]),
                                    tle[0:Lc, 0:Lc].unsqueeze(1).broadcast_to([Lc, 12, Lc]), ALU.mult),
                                    reads=["aall", "consts"], writes=["arhs"])
                                for hf in range(2):
                                    P.add("pe", lambda e, Lc=Lc, hf=hf: e.matmul(
                                        bigA[0:Lc, hf, 0:6 * Lc], ustrict[0:Lc, 0:Lc], arhs[0:Lc, hf * 6 * Lc:(hf + 1) * 6 * Lc],
                                        start=True, stop=True),
                                        reads=["arhs", "consts"], writes=["bigA"])
                                P.add("pe", lambda e, Lc=Lc, ci=ci: e.matmul(
                                    misc[0:Lc, 64:76], tle[0:Lc, 0:Lc], aall[0:Lc, ci, g12:g12 + 12], start=True, stop=True),
                                    reads=["aall", "consts"], writes=["misc"])
                                P.add("pe", lambda e, Lc=Lc, ci=ci: e.matmul(
                                    misc[:, 76:88], ones[0:Lc, :], aall[0:Lc, ci, g12:g12 + 12], start=True, stop=True),
                                    reads=["aall", "ones"], writes=["misc"])
                                stage(5.4)
                                P.add("act", lambda e, Lc=Lc: e.activation(
                                    out=E[0:Lc, 0:12 * Lc].rearrange("t (a b) -> t a b", a=2), in_=bigA[0:Lc, :, 0:6 * Lc], func=AF.Exp),
                                    reads=["bigA"], writes=["E"])
                                P.add("act", lambda e, Lc=Lc: e.activation(out=ea[0:Lc, :], in_=misc[0:Lc, 64:76], func=AF.Exp),
                                      reads=["misc"], writes=["ea"])
                                P.add("act", lambda e: e.activation(out=cd[:, :], in_=misc[:, 76:88], func=AF.Exp),
                                      reads=["misc"], writes=["cd"])
                                P.add("dve", lambda e, Lc=Lc: e.tensor_tensor(
                                    MT[0:Lc, 0:12 * Lc].rearrange("t (h l) -> t h l", h=12),
                                    E[0:Lc, 0:12 * Lc].rearrange("t (h l) -> t h l", h=12),
                                    CBm[0:Lc, 0:Lc].unsqueeze(1).broadcast_to([Lc, 12, Lc]), ALU.mult),
                                    reads=["E", "CBm"], writes=["MT"])
                                P.add("dve", lambda e, Lc=Lc: e.tensor_tensor(
                                    xdd[0:Lc, :].rearrange("l (h p) -> l h p", h=12),
                                    xdt[0:Lc, :].rearrange("l (h p) -> l h p", h=12),
                                    E[0:Lc, 0:12 * Lc].rearrange("t (h l) -> t h l", h=12)[:, :, Lc - 1:Lc].broadcast_to([Lc, 12, 64]),
                                    ALU.mult),
                                    reads=["xdt", "E"], writes=["xdd"])
                                stage(5.5)
                                for hf in range(2):
                                    P.add("pe", lambda e, Lc=Lc, hf=hf, cols=cols: e.matmul(
                                        bigB[0:Lc, hf, 0:384], Cf[:, cols], S_bf[:, hf * 384:(hf + 1) * 384], start=True, stop=True),
                                        reads=["Cf", "S_bf"], writes=["bigB"])
                                for h in range(12):
                                    P.add("pe", lambda e, Lc=Lc, h=h: e.matmul(
                                        bigA[0:Lc, h // 6, (h % 6) * 64:(h % 6 + 1) * 64], MT[0:Lc, h * Lc:(h + 1) * Lc],
                                        xdt[0:Lc, h * 64:(h + 1) * 64], start=True, stop=True),
                                        reads=["MT", "xdt", "E"], writes=["bigA"])
                                y1v = lambda Lc=Lc: y1[0:Lc, :].rearrange("l (a b p) -> l a b p", a=2, b=6)
                                P.add("dve", lambda e, Lc=Lc, y1v=y1v: e.tensor_tensor(
                                    y1v(), bigB[0:Lc, :, 0:384].rearrange("l a (b p) -> l a b p", b=6),
                                    ea[0:Lc, :].rearrange("l (a b) -> l a b", a=2).unsqueeze(3).broadcast_to([Lc, 2, 6, 64]), ALU.mult),
                                    reads=["bigB", "ea"], writes=["y1"])
                                P.add("dve", lambda e, Lc=Lc, y1v=y1v: e.tensor_tensor(
                                    y1v(), y1v(), bigA[0:Lc, :, 0:384].rearrange("l a (b p) -> l a b p", b=6), ALU.add),
                                    reads=["bigA", "y1"], writes=["y1"])
                                P.add("dve", lambda e, Lc=Lc: e.tensor_tensor(
                                    arhs[0:Lc, :].rearrange("l (h p) -> l h p", h=12),
                                    xs_tm[0:Lc, :].rearrange("l (h p) -> l h p", h=12),
                                    dsk[0:Lc, g12:g12 + 12].unsqueeze(2).broadcast_to([Lc, 12, 64]), ALU.mult),
                                    reads=["xs_tm", "lp"], writes=["arhs"])
                                P.add("dve", lambda e, Lc=Lc: e.tensor_tensor(y1[0:Lc, :], y1[0:Lc, :], arhs[0:Lc, :], ALU.add),
                                      reads=["y1", "arhs"], writes=["y1"])
                                stage(5.6)
                                for j in range(6):
                                    P.add("pe", lambda e, Lc=Lc, j=j: e.transpose(
                                        bigA[:, 0, j * Lc:(j + 1) * Lc], y1[0:Lc, j * 128:(j + 1) * 128], ident[0:Lc, 0:Lc]),
                                        reads=["y1", "ident"], writes=["bigA"])
                                P.add("dve", lambda e, Lc=Lc, cols=cols: e.tensor_tensor(
                                    yg[:, :, 0:Lc], bigA[:, 0, 0:6 * Lc].rearrange("p (j l) -> p j l", j=6), mixedT[:, 16 + 6 * g:22 + 6 * g, cols], ALU.mult),
                                    reads=["bigA", "zs"], writes=["yg"])
                                P.add("act", lambda e, Lc=Lc: e.activation(out=sq[:, :, 0:Lc], in_=yg[:, :, 0:Lc], func=AF.Square),
                                      reads=["yg"], writes=["sq"])
                                for j in range(6):
                                    P.add("pe", lambda e, Lc=Lc, j=j: e.matmul(
                                        misc[:, 128:128 + Lc], ones[:, :], sq[:, j, 0:Lc], start=(j == 0), stop=(j == 5)),
                                        reads=["sq", "ones"], writes=["misc"])
                                P.add("act", lambda e, Lc=Lc: e.activation(
                                    out=rstd[:, 0:Lc], in_=misc[:, 128:128 + Lc], func=AF.Ln, bias=RMS_EPS, scale=1.0 / 768.0),
                                    reads=["misc"], writes=["rstd"])
                                P.add("act", lambda e, Lc=Lc: e.activation(out=rstd[:, 0:Lc], in_=rstd[:, 0:Lc], func=AF.Exp, scale=-0.5),
                                      reads=["rstd"], writes=["rstd"])
                                P.add("dve", lambda e, Lc=Lc, cols=cols, g=g: e.tensor_tensor(
                                    mixedT[:, 16 + 6 * g:22 + 6 * g, cols], yg[:, :, 0:Lc],
                                    rstd[:, 0:Lc].unsqueeze(1).broadcast_to([128, 6, Lc]), ALU.mult),
                                    reads=["yg", "rstd"], writes=["mxs%d" % g])
                                stage(5.7)
                                for hf in range(2):
                                    P.add("pe", lambda e, Lc=Lc, hf=hf: e.matmul(
                                        bigB[:, hf, 0:384], B_tm[0:Lc, :], xdd[0:Lc, hf * 384:(hf + 1) * 384], start=True, stop=True),
                                        reads=["B_tm", "xdd", "y1"], writes=["bigB"])
                                stage(5.72)
                                P.add("dve", lambda e: e.tensor_tensor(
                                    S[:].rearrange("n (h p) -> n h p", h=12), S[:].rearrange("n (h p) -> n h p", h=12),
                                    cd[:, :].unsqueeze(2).broadcast_to([128, 12, 64]), ALU.mult),
                                    reads=["S", "cd"], writes=["S"])
                                stage(5.74)
                                P.add("dve", lambda e: e.tensor_tensor(
                                    S[:].rearrange("n (a b) -> n a b", a=2), S[:].rearrange("n (a b) -> n a b", a=2),
                                    bigB[:, :, 0:384], ALU.add),
                                    reads=["S", "bigB"], writes=["S"])
                                stage(5.76)
                                P.add("act", lambda e: e.copy(S_bf[:], S[:]), reads=["S"], writes=["S_bf"])
                                stage(5.8)
                                if kind == "s":
                                    store_state(l, g, nssm_s[l, sg], S, stg)
                            if kind == "p":
                                if last:
                                    store_state(l, g, nssm_p[l], S, stg)
                                else:
                                    P.add("sp", lambda e, g=g: e.dma_start(out=stscr[g], in_=S[:]),
                                          reads=["S"], writes=["stscr%d" % g], dma=True)
                            for j in range(6):
                                P.add("dve", lambda e, j=j, g=g: e.tensor_scalar(
                                    mixedT[:, 16 + 6 * g + j, 0:NT], mixedT[:, 16 + 6 * g + j, 0:NT],
                                    nw[:, 6 * g + j:6 * g + j + 1], None, ALU.mult),
                                    reads=["mxs%d" % g, "lp"], writes=["mxs%d" % g])
                    barrier()

                stage(6)
                with contextlib.ExitStack() as cst2:
                    og = cst2.enter_context(SBT("og", [128, TB], F32))
                    ost = cst2.enter_context(SBT("ost", [128, 4, 128], F32))
                    mxkeys = ["mx%d" % i for i in range(16)] + ["mxs%d" % i for i in range(8)]
                    odst = oscr.rearrange("(tt p) d -> p tt d", p=128)
                    for db in range(32):
                        b, view = load_slab_in(w_out[l], [(db * 128, 128, 0)], nkt=64, width=128)
                        i = next_pj()
                        for kt in range(64):
                            P.add("pe", lambda e, i=i, kt=kt, view=view: e.matmul(
                                pj[i][:, 0:NT], view[:, kt, :], mixedT[:, kt, 0:NT], start=(kt == 0), stop=(kt == 63)),
                                reads=["wsl%d" % b] + (mxkeys if kt == 0 else []), writes=["pj%d" % i])
                        for si in range(nseg):
                            r = segr[si]
                            c0, c1 = (0, NT) if kind == "p" else (si * Ls, (si + 1) * Ls)
                            P.add("act", lambda e, i=i, c0=c0, c1=c1, r=r, db=db: e.activation(
                                out=og[:, c0:c1], in_=pj[i][:, c0:c1], func=AF.Copy, scale=modT[:, 64 + db, r:r + 1]),
                                reads=["pj%d" % i, "modT"], writes=["og"])
                        for tt in range(ntt):
                            P.add("pe", lambda e, tt=tt: e.transpose(
                                bigA[0:npt, 0, tt * 128:(tt + 1) * 128], og[:, tt * 128:tt * 128 + npt], ident[:, :]),
                                reads=["og", "ident"], writes=["bigA"])
                        P.add("dve", lambda e: e.tensor_copy(
                            ost[0:npt, 0:ntt, :], bigA[0:npt, 0, 0:ntt * 128].rearrange("p (t d) -> p t d", t=ntt)),
                            reads=["bigA"], writes=["ost"])
                        P.add("sp", lambda e, db=db: e.dma_start(
                            out=odst[0:npt, 0:ntt, db * 128:(db + 1) * 128], in_=ost[0:npt, 0:ntt, :]),
                            reads=["ost"], writes=["oscr"], dma=True)
                barrier()

            stage(7)
            with contextlib.ExitStack() as dst_:
                def DB(name, shape, dt=F32):
                    return dst_.enter_context(SBT(name, list(shape), dt))
                X = DB("XD", [128, D])
                O = DB("OD", [128, D])
                G = DB("GD", [128, D])
                Bb = DB("BD", [128, D])
                stt = DB("sttD", [128, 8, 6])
                mv = DB("mvD", [128, 2])
                rs = DB("rsD", [128, 1])
                nb = DB("nbD", [128, 1])
                P.add("sp", lambda e: e.dma_start(out=G[:], in_=lng_bc[l]), writes=["GD"], dma=True)
                P.add("sp", lambda e: e.dma_start(out=Bb[:], in_=lnb_bc[l]), writes=["BD"], dma=True)
                for tt in range(ntt):
                    r0 = tt * 128
                    P.add("sp", lambda e, r0=r0: e.dma_start(out=O[0:npt, :], in_=oscr[r0:r0 + npt, :]),
                          reads=["oscr"], writes=["OD"], dma=True)
                    P.add("sp", lambda e, r0=r0: e.dma_start(out=X[0:npt, :], in_=xsrc[r0:r0 + npt, :]),
                          reads=[blk["srck"]], writes=["XD"], dma=True)
                    P.add("dve", lambda e: e.scalar_tensor_tensor(X[0:npt, :], X[0:npt, :], ALPHA, O[0:npt, :], ALU.mult, ALU.add),
                          reads=["XD", "OD"], writes=["XD"])
                    for c in range(8):
                        P.add("dve", lambda e, c=c: e.bn_stats(stt[0:npt, c, :], X[0:npt, c * 512:(c + 1) * 512]),
                              reads=["XD"], writes=["sttD"])
                    P.add("dve", lambda e: e.bn_aggr(mv[0:npt, :], stt[0:npt, :, :]), reads=["sttD"], writes=["mvD"])
                    P.add("act", lambda e: e.activation(out=rs[0:npt, :], in_=mv[0:npt, 1:2], func=AF.Ln, bias=LN_EPS),
                          reads=["mvD"], writes=["rsD"])
                    P.add("act", lambda e: e.activation(out=rs[0:npt, :], in_=rs[0:npt, :], func=AF.Exp, scale=-0.5),
                          reads=["rsD"], writes=["rsD"])
                    P.add("dve", lambda e: e.scalar_tensor_tensor(nb[0:npt, :], mv[0:npt, 0:1], -1.0, rs[0:npt, :], ALU.mult, ALU.mult),
                          reads=["mvD", "rsD"], writes=["nbD"])
                    P.add("act", lambda e: e.activation(out=X[0:npt, :], in_=X[0:npt, :], func=AF.Identity,
                                                        bias=nb[0:npt, :], scale=rs[0:npt, :]),
                          reads=["XD", "rsD", "nbD"], writes=["XD"])
                    P.add("dve", lambda e: e.tensor_tensor(X[0:npt, :], X[0:npt, :], G[0:npt, :], ALU.mult),
                          reads=["XD", "GD"], writes=["XD"])
                    P.add("dve", lambda e: e.tensor_tensor(X[0:npt, :], X[0:npt, :], Bb[0:npt, :], ALU.add),
                          reads=["XD", "BD"], writes=["XD"])
                    P.add("sp", lambda e, r0=r0: e.dma_start(out=xdst[r0:r0 + npt, :], in_=X[0:npt, :]),
                          reads=["XD"], writes=[blk["dstk"]], dma=True)

        def store_state(l, g, dstap, S, stg):
            for j in range(6):
                P.add("pe", lambda e, j=j: e.transpose(
                    bigB[:, j // 3, (j % 3) * 128:(j % 3 + 1) * 128], S[:, j * 128:(j + 1) * 128], ident[:, :]),
                    reads=["S", "ident"], writes=["bigB"])
            stage(5.82)
            P.add("act", lambda e: e.copy(stg[:].rearrange("q (a b) n -> q a (b n)", a=2), bigB[:, :, 0:384]),
                  reads=["bigB"], writes=["stg"])
            stage(5.84)
            dd = dstap[768 * g:768 * (g + 1), :].rearrange("(j q) n -> q j n", q=128)
            P.add("sp", lambda e, dd=dd: e.dma_start(out=dd, in_=stg[:]), reads=["stg"], dma=True)
            stage(5.86)

        def store_hist(l, seg, pool_dst, conv_dst):
            stage(8)
            barrier()
            with SBT("hstg", [16, 1024], F32) as hstg:
                for half in range(2):
                    for i in range(8):
                        ct = half * 8 + i
                        P.add("pe", lambda e, i=i, ct=ct: e.transpose(
                            bigA[0:15, i // 4, (i % 4) * 128:(i % 4 + 1) * 128], uhist[:, ct, seg, :], ident[:, :]),
                            reads=["uhist", "ident"], writes=["bigA"])
                    P.add("act", lambda e: e.copy(hstg[0:15, :].rearrange("t (a b) -> t a b", a=2), bigA[0:15, :, :]),
                          reads=["bigA"], writes=["hstg"])
                    P.add("sp", lambda e, half=half: e.dma_start(out=pool_dst[:, half * 1024:(half + 1) * 1024], in_=hstg[0:15, :]),
                          reads=["hstg"], dma=True)
                for pc in range(8):
                    for i in range(8):
                        ct = pc * 8 + i
                        P.add("pe", lambda e, i=i, ct=ct: e.transpose(
                            bigA[0:3, i // 4, (i % 4) * 128:(i % 4 + 1) * 128], chist[:, ct, seg, :], ident[:, :]),
                            reads=["chist", "ident"], writes=["bigA"])
                    P.add("act", lambda e: e.copy(hstg[0:3, :].rearrange("t (a b) -> t a b", a=2), bigA[0:3, :, :]),
                          reads=["bigA"], writes=["hstg"])
                    P.add("sp", lambda e, pc=pc: e.dma_start(out=conv_dst[:, pc * 1024:(pc + 1) * 1024], in_=hstg[0:3, :]),
                          reads=["hstg"], dma=True)
            barrier()

        def load_hist_sample(l):
            barrier()
            with SBT("hstg2", [16, 1024], F32) as hstg:
                for seg in range(2):
                    for half in range(2):
                        P.add("sp", lambda e, seg=seg, half=half: e.dma_start(
                            out=hstg[0:15, :], in_=spool[l, seg, :, half * 1024:(half + 1) * 1024]), writes=["hstg"], dma=True)
                        for i in range(8):
                            P.add("pe", lambda e, i=i: e.transpose(
                                misc[:, i * 15:(i + 1) * 15], hstg[0:15, i * 128:(i + 1) * 128], ident[0:15, 0:15]),
                                reads=["hstg", "ident"], writes=["misc"])
                        P.add("act", lambda e, seg=seg, half=half: e.copy(
                            uhist[:, half * 8:(half + 1) * 8, seg, :], misc[:, 0:120].rearrange("p (i t) -> p i t", i=8)),
                            reads=["misc"], writes=["uhist"])
                    for pc in range(8):
                        P.add("sp", lambda e, seg=seg, pc=pc: e.dma_start(
                            out=hstg[0:3, :], in_=sconv[l, seg, :, pc * 1024:(pc + 1) * 1024]), writes=["hstg"], dma=True)
                        for i in range(8):
                            P.add("pe", lambda e, i=i: e.transpose(
                                misc[:, i * 3:(i + 1) * 3], hstg[0:3, i * 128:(i + 1) * 128], ident[0:3, 0:3]),
                                reads=["hstg", "ident"], writes=["misc"])
                        P.add("act", lambda e, seg=seg, pc=pc: e.copy(
                            chist[:, pc * 8:(pc + 1) * 8, seg, :], misc[:, 0:24].rearrange("p (i t) -> p i t", i=8)),
                            reads=["misc"], writes=["chist"])
            barrier()

        setup()
        for l in LAYERS:
            lfirst, llast = (l == LAYERS[0]), (l == LAYERS[-1])
            layer_setup(l)
            for bi in range(NPBLK):
                rows = slice(bi * TB, (bi + 1) * TB)
                blk = dict(kind="p", NT=TB, nseg=1, Ls=TB, first=(bi == 0), last=(bi == NPBLK - 1),
                           src=(xp if lfirst else x1p)[rows, :], dst=(yp if llast else x1p)[rows, :],
                           srck=("xp" if lfirst else "x1p%d" % bi), dstk=("yp%d" % bi if llast else "x1p%d" % bi))
                process_block(l, blk)
            if NPBLK:
                store_hist(l, 0, npool_p[l], nconv_p[l])
            if DO_SAMPLE:
                load_hist_sample(l)
                blk = dict(kind="s", NT=64, nseg=2, Ls=32,
                           src=(xs if lfirst else x1s), dst=(ys if llast else x1s),
                           srck=("xs" if lfirst else "x1s"), dstk=("ys" if llast else "x1s"))
                process_block(l, blk)
                for seg in range(2):
                    store_hist(l, seg, npool_s[l, seg], nconv_s[l, seg])
        P.emit()
    return nc


def _fm(v, ntile):
    sh = v.shape[:-1]
    return np.ascontiguousarray(np.moveaxis(v.reshape(sh + (ntile, 128)), -1, -2))


def make_in_maps(inp, n_cores=8):
    f = lambda a: np.ascontiguousarray(np.asarray(a, dtype=np.float32))
    x_prompt, x_sample = f(inp["x_prompt"]), f(inp["x_sample"])
    c_prompt, c_sample = f(inp["c_prompt"]), f(inp["c_sample"])
    state_pool, state_conv, state_ssm = f(inp["state_pool"]), f(inp["state_conv"]), f(inp["state_ssm"])
    w_ada, w_in, w_pool, w_out = f(inp["w_ada"]), f(inp["w_in"]), f(inp["w_pool"]), f(inp["w_out"])
    shared = {
        "w_ada": w_ada, "w_in": w_in, "w_pool": w_pool, "w_out": w_out,
        "badaT": _fm(f(inp["b_ada"]), 96),
        "pscT": _fm(f(inp["pool_scale"]), 16),
        "cwT": np.ascontiguousarray(np.transpose(f(inp["conv_w"]).reshape(2, 4, 64, 128), (0, 3, 2, 1))),
        "cbT": _fm(f(inp["conv_b"]), 64),
        "nwT": _fm(f(inp["ssd_norm_w"]), 48),
        "h96": np.ascontiguousarray(np.broadcast_to(
            np.stack([f(inp["dt_bias"]), f(inp["a_log"]), f(inp["d_skip"])], axis=1)[:, :, None, :], (2, 3, 128, 96))),
        "lng_bc": np.ascontiguousarray(np.broadcast_to(f(inp["ln_g"])[:, None, :], (2, 128, D))),
        "lnb_bc": np.ascontiguousarray(np.broadcast_to(f(inp["ln_b"])[:, None, :], (2, 128, D))),
        "c_ident": np.eye(128, dtype=np.float32),
        "c_ustrict": np.ascontiguousarray((np.arange(64)[:, None] > np.arange(64)[None, :]).astype(np.float32)),
        "c_tle": np.ascontiguousarray((np.arange(64)[:, None] <= np.arange(64)[None, :]).astype(np.float32)),
        "c_invc": np.ascontiguousarray(np.broadcast_to(
            np.stack([1.0 / np.minimum(np.arange(16) + 1, w) for w in (2, 4, 8, 16)]).astype(np.float32)[None], (128, 4, 16))),
    }
    maps = []
    for c in range(n_cores):
        pc = c % 4
        crow = np.stack([c_prompt[pc], c_sample[2 * c], c_sample[2 * c + 1]], axis=0)
        m = dict(shared)
        m["xp"] = x_prompt[pc]
        m["xs"] = np.ascontiguousarray(x_sample[2 * c:2 * c + 2].reshape(64, D))
        m["cT"] = np.ascontiguousarray(np.transpose(crow.reshape(3, NKT, 128), (2, 1, 0)))
        m["spool"] = np.ascontiguousarray(state_pool[:, 2 * c:2 * c + 2])
        m["sconv"] = np.ascontiguousarray(state_conv[:, 2 * c:2 * c + 2])
        m["sssm"] = np.ascontiguousarray(state_ssm[:, 2 * c:2 * c + 2].reshape(2, 2, W_SSD, 128))
        maps.append(m)
    return maps


def assemble(results):
    yp = np.stack([results[c]["yp"] for c in range(4)], axis=0)
    ys = np.concatenate([results[c]["ys"].reshape(2, 32, D) for c in range(8)], axis=0)
    npp = np.stack([results[c]["npool_p"] for c in range(4)], axis=1)
    ncp = np.stack([results[c]["nconv_p"] for c in range(4)], axis=1)
    nsp = np.stack([results[c]["nssm_p"].reshape(2, 96, 64, 128) for c in range(4)], axis=1)
    nps = np.concatenate([results[c]["npool_s"] for c in range(8)], axis=1)
    ncs = np.concatenate([results[c]["nconv_s"] for c in range(8)], axis=1)
    nss = np.concatenate([results[c]["nssm_s"].reshape(2, 2, 96, 64, 128) for c in range(8)], axis=1)
    return tuple(np.ascontiguousarray(a, dtype=np.float32) for a in (yp, ys, npp, ncp, nsp, nps, ncs, nss))


def kernel(**inputs):
    nc = build()
    in_maps = make_in_maps(inputs)
    res = run_bass_kernel_spmd(nc, in_maps, core_ids=list(range(8)))
    return assemble(res.results)
```

```python
import contextlib
import types
import numpy as np
import concourse.bass as bass
import concourse.mybir as mybir
from concourse.bass_utils import run_bass_kernel_spmd

F32 = mybir.dt.float32
BF16 = mybir.dt.bfloat16
AF = mybir.ActivationFunctionType
ALU = mybir.AluOpType

ENGS = ("pe", "act", "dve", "pool", "sp")
NDSEM = 8

D = 4096
NKT = 32
IN_DIM = 18528
W_POOL = 2048
W_SSD = 6144
CONV_DIM = 8192
ALPHA = 4.0 ** 0.25
LN_EPS = 1e-5
RMS_EPS = 1e-5
SEQ = 2048
TB = 512
NPB = SEQ // TB


class Op:
    __slots__ = ("eng", "fn", "dma", "deps", "signals", "count", "dsem", "dtarget", "idx", "qidx")

    def __init__(self, eng, fn, dma):
        self.eng = eng
        self.fn = fn
        self.dma = dma
        self.deps = set()
        self.signals = False
        self.count = 0
        self.dsem = None
        self.dtarget = 0
        self.idx = -1
        self.qidx = -1


def _freeze(fn):
    if not fn.__closure__:
        return fn
    cells = []
    for c in fn.__closure__:
        try:
            cells.append(types.CellType(c.cell_contents))
        except ValueError:
            cells.append(c)
    return types.FunctionType(fn.__code__, fn.__globals__, fn.__name__, fn.__defaults__, tuple(cells))


class Prog:
    def __init__(self, nc, same_engine_sync=True):
        self.nc = nc
        self.ops = []
        self.res = {}
        self.ndma = {e: 0 for e in ENGS}
        self.dma_ops = {e: [] for e in ENGS}
        self.same_engine_sync = same_engine_sync
        self.enabled = True
        self.maxops = 10 ** 9

    def add(self, eng, fn, reads=(), writes=(), dma=False, ph=True):
        if not self.enabled or len(self.ops) >= self.maxops:
            return None
        op = Op(eng, _freeze(fn), dma)
        op.idx = len(self.ops)
        reads = list(reads)
        if ph:
            reads.append("PH")
        for k in reads:
            r = self.res.setdefault(k, [None, []])
            if r[0] is not None:
                op.deps.add(r[0])
        for k in writes:
            r = self.res.setdefault(k, [None, []])
            if r[0] is not None:
                op.deps.add(r[0])
            op.deps.update(r[1])
        for k in reads:
            self.res[k][1].append(op.idx)
        for k in writes:
            self.res[k] = [op.idx, []]
        if dma:
            j = self.ndma[eng]
            op.qidx = j
            self.ndma[eng] += 1
            if j >= NDSEM:
                op.deps.add(self.dma_ops[eng][j - NDSEM].idx)
            self.dma_ops[eng].append(op)
        op.deps.discard(op.idx)
        self.ops.append(op)
        return op

    def emit(self):
        nc = self.nc
        ops = self.ops
        ses = self.same_engine_sync

        def skip(p, op):
            return (not p.dma) and p.eng == op.eng and (p.eng == "pe" or not ses) and not op.dma

        for op in ops:
            latest = {}
            keep = set()
            for d in op.deps:
                p = ops[d]
                if p.dma:
                    keep.add(d)
                elif latest.get(p.eng, -1) < d:
                    latest[p.eng] = d
            keep.update(latest.values())
            op.deps = keep
        for op in ops:
            for d in op.deps:
                p = ops[d]
                if p.dma or skip(p, op):
                    continue
                p.signals = True
        cnt = {e: 0 for e in ENGS}
        for op in ops:
            if not op.dma and op.signals:
                cnt[op.eng] += 1
                op.count = cnt[op.eng]
        with contextlib.ExitStack() as st:
            esem = {e: st.enter_context(nc.semaphore("s_" + e)) for e in ENGS}
            dsem = {}
            for e in ENGS:
                if self.ndma[e]:
                    dsem[e] = [st.enter_context(nc.semaphore("d_%s%d" % (e, i))) for i in range(NDSEM)]
            for e in ENGS:
                for op in self.dma_ops[e]:
                    op.dsem = dsem[e][op.qidx % NDSEM]
                    op.dtarget = 16 * (op.qidx // NDSEM + 1)
            block = st.enter_context(nc.Block())
            byeng = {e: [op for op in ops if op.eng == e] for e in ENGS}

            def run(e, eng):
                waited = {}
                for op in byeng[e]:
                    need = {}
                    for d in op.deps:
                        p = ops[d]
                        if p.dma:
                            key = ("d", p.eng, p.qidx % NDSEM)
                            sem, val = p.dsem, p.dtarget
                        else:
                            if skip(p, op):
                                continue
                            key = ("e", p.eng)
                            sem, val = esem[p.eng], p.count
                        if need.get(key, (None, 0))[1] < val:
                            need[key] = (sem, val)
                    for key, (sem, val) in need.items():
                        if waited.get(key, 0) >= val:
                            continue
                        waited[key] = val
                        eng.wait_ge(sem, val)
                    ins = op.fn(eng)
                    if op.dma:
                        ins.then_inc(op.dsem, 16)
                    elif op.signals:
                        ins.then_inc(esem[e], 1)
                if e == "sp":
                    for q in ENGS:
                        n = self.ndma[q]
                        for i in range(min(n, NDSEM)):
                            k = (n - i + NDSEM - 1) // NDSEM
                            eng.wait_ge(dsem[q][i], 16 * k)

            @block.tensor
            def _(eng):
                run("pe", eng)

            @block.scalar
            def _(eng):
                run("act", eng)

            @block.vector
            def _(eng):
                run("dve", eng)

            @block.gpsimd
            def _(eng):
                run("pool", eng)

            @block.sync
            def _(eng):
                run("sp", eng)


def build(cfg=None):
    cfg = cfg or {}
    LAYERS = cfg.get("layers", [0, 1])
    NPBLK = cfg.get("npblk", NPB)
    DO_SAMPLE = cfg.get("sample", True)

    nc = bass.Bass("TRN2", target_bir_lowering=False)

    FAKEW = cfg.get("fakew", False)

    def din(name, shape):
        if FAKEW and name in ("w_ada", "w_in", "w_out"):
            return None
        return nc.dram_tensor(name, list(shape), F32, kind="ExternalInput").ap()

    def dout(name, shape):
        return nc.dram_tensor(name, list(shape), F32, kind="ExternalOutput").ap()

    xp = din("xp", [SEQ, D])
    xs = din("xs", [64, D])
    cT = din("cT", [128, NKT, 3])
    spool = din("spool", [2, 2, 15, W_POOL])
    sconv = din("sconv", [2, 2, 3, CONV_DIM])
    sssm = din("sssm", [2, 2, W_SSD, 128])
    w_ada = din("w_ada", [2, D, 3 * D])
    badaT = din("badaT", [2, 128, 96])
    w_in = din("w_in", [2, D, IN_DIM])
    w_pool = din("w_pool", [2, 4, 512, 512])
    w_out = din("w_out", [2, 2 * D, D])
    pscT = din("pscT", [2, 128, 16])
    cwT = din("cwT", [2, 128, 64, 4])
    cbT = din("cbT", [2, 128, 64])
    nwT = din("nwT", [2, 128, 48])
    h96 = din("h96", [2, 3, 128, 96])
    lng_bc = din("lng_bc", [2, 128, D])
    lnb_bc = din("lnb_bc", [2, 128, D])
    c_ident = din("c_ident", [128, 128])
    c_ustrict = din("c_ustrict", [64, 64])
    c_tle = din("c_tle", [64, 64])
    c_invc = din("c_invc", [128, 4, 16])

    w_fake = nc.dram_tensor("w_fake", [8192, 512], F32, kind="ExternalInput").ap() if FAKEW else None
    if FAKEW:
        w_ada = w_in = w_out = [w_fake, w_fake]

    yp = dout("yp", [SEQ, D])
    ys = dout("ys", [64, D])
    npool_p = dout("npool_p", [2, 15, W_POOL])
    nconv_p = dout("nconv_p", [2, 3, CONV_DIM])
    nssm_p = dout("nssm_p", [2, W_SSD, 128])
    npool_s = dout("npool_s", [2, 2, 15, W_POOL])
    nconv_s = dout("nconv_s", [2, 2, 3, CONV_DIM])
    nssm_s = dout("nssm_s", [2, 2, W_SSD, 128])

    x1p = nc.dram_tensor("x1p", [SEQ, D], F32).ap()
    x1s = nc.dram_tensor("x1s", [64, D], F32).ap()
    oscr = nc.dram_tensor("oscr", [TB, D], F32).ap()
    stscr = nc.dram_tensor("stscr", [8, 128, 768], F32).ap()

    _uid = [0]

    def SBT(name, shape, dt=F32):
        _uid[0] += 1
        return nc.sbuf_tensor("%s_%d" % (name, _uid[0]), list(shape), dt)

    with contextlib.ExitStack() as st:
        def SB(name, shape, dt=F32):
            return st.enter_context(SBT(name, list(shape), dt))

        def PS(name, shape, dt=F32):
            return st.enter_context(nc.psum_tensor(name, list(shape), dt))

        wsl = [SB("wsl%d" % i, [128, 8192], BF16) for i in range(2)]
        ident = SB("ident", [128, 128])
        identb = SB("identb", [128, 128], BF16)
        ustrict = SB("ustrict", [64, 64])
        tle = SB("tle", [64, 64])
        ones = SB("ones", [128, 128])
        onesb = SB("onesb", [128, 128], BF16)
        invc = SB("invc", [128, 4, 16])
        cTs = SB("cTs", [128, NKT, 3])
        scT = SB("scT", [128, NKT, 3], BF16)
        modT = SB("modT", [128, 96, 3])
        onep = SB("onep", [128, 32, 3])
        bada = SB("bada", [128, 96])
        psc = SB("psc", [128, 16])
        cw = SB("cw", [128, 64, 4])
        cb = SB("cb", [128, 64])
        nw = SB("nw", [128, 48])
        dtb = SB("dtb", [128, 96])
        Abc = SB("Abc", [128, 96])
        dsk = SB("dsk", [128, 96])
        uhist = SB("uhist", [128, 16, 2, 15])
        chist = SB("chist", [128, 64, 2, 3])
        scr1 = SB("scr1", [128, 8])

        pj = [PS("pj%d" % i, [128, 512]) for i in range(2)]
        misc = PS("misc", [128, 512])
        tpx = PS("tpx", [128, 1024], BF16)
        bigA = PS("bigA", [128, 2, 512])
        bigB = PS("bigB", [128, 2, 512])

        P = Prog(nc)
        P.maxops = cfg.get("maxops", 10 ** 9)
        state = {"slab": 0, "pj": 0, "ev": 0}

        STAGE = cfg.get("stage", 99)

        def stage(k):
            if k > STAGE:
                P.enabled = False

        def barrier():
            P.add("dve", lambda e: e.memset(scr1[:, 0:1], 0.0), writes=["PH"], ph=False)

        def load_slab_in(wsrc, pieces, nkt=NKT, width=256):
            b = state["slab"] % 2
            state["slab"] += 1
            view = wsl[b][:, 0:nkt * width].rearrange("p (k c) -> p k c", k=nkt)
            if FAKEW:
                wsrc = w_fake[0:nkt * 128, :]
                pieces = [(0, ncol, dc) for (c0, ncol, dc) in pieces]
            src = wsrc.rearrange("(kt p) c -> p kt c", p=128)
            for (c0, ncol, dc) in pieces:
                for q in range(nkt // 8):
                    P.add("pool", lambda e, c0=c0, ncol=ncol, dc=dc, q=q, view=view, src=src: e.dma_start(
                        out=view[:, q * 8:(q + 1) * 8, dc:dc + ncol], in_=src[:, q * 8:(q + 1) * 8, c0:c0 + ncol]),
                        writes=["wsl%d" % b], dma=True, ph=False)
            return b, view

        def next_pj():
            i = state["pj"] % 2
            state["pj"] += 1
            return i

        def setup():
            P.add("sp", lambda e: e.dma_start(out=ident[:], in_=c_ident[:, :]), writes=["ident"], dma=True)
            P.add("sp", lambda e: e.dma_start(out=ustrict[:], in_=c_ustrict[:, :]), writes=["consts"], dma=True)
            P.add("sp", lambda e: e.dma_start(out=tle[:], in_=c_tle[:, :]), writes=["consts"], dma=True)
            P.add("sp", lambda e: e.dma_start(out=invc[:], in_=c_invc[:, :, :]), writes=["consts"], dma=True)
            P.add("sp", lambda e: e.dma_start(out=cTs[:], in_=cT[:, :, :]), writes=["cTs"], dma=True)
            P.add("dve", lambda e: e.tensor_copy(identb[:], ident[:]), reads=["ident"], writes=["identb"])
            P.add("dve", lambda e: e.memset(ones[:], 1.0), writes=["ones"])
            P.add("dve", lambda e: e.tensor_copy(onesb[:], ones[:]), reads=["ones"], writes=["onesb"])
            P.add("act", lambda e: e.activation(out=scT[:], in_=cTs[:], func=AF.Silu), reads=["cTs"], writes=["scT"])

        def layer_setup(l):
            stage(1)
            barrier()
            P.add("sp", lambda e: e.dma_start(out=bada[:], in_=badaT[l]), writes=["bada"], dma=True)
            P.add("sp", lambda e: e.dma_start(out=psc[:], in_=pscT[l]), writes=["lp"], dma=True)
            P.add("sp", lambda e: e.dma_start(out=cw[:], in_=cwT[l]), writes=["lp"], dma=True)
            P.add("sp", lambda e: e.dma_start(out=cb[:], in_=cbT[l]), writes=["lp"], dma=True)
            P.add("sp", lambda e: e.dma_start(out=nw[:], in_=nwT[l]), writes=["lp"], dma=True)
            P.add("sp", lambda e: e.dma_start(out=dtb[:], in_=h96[l, 0]), writes=["lp"], dma=True)
            P.add("sp", lambda e: e.dma_start(out=Abc[:], in_=h96[l, 1]), writes=["Abc"], dma=True)
            P.add("sp", lambda e: e.dma_start(out=dsk[:], in_=h96[l, 2]), writes=["lp"], dma=True)
            P.add("act", lambda e: e.activation(out=Abc[:], in_=Abc[:], func=AF.Exp), reads=["Abc"], writes=["Abc"])
            P.add("dve", lambda e: e.tensor_scalar(Abc[:], Abc[:], -1.0, None, ALU.mult), reads=["Abc"], writes=["Abc"])
            P.add("dve", lambda e: e.memset(uhist[:], 0.0), writes=["uhist"])
            P.add("dve", lambda e: e.memset(chist[:], 0.0), writes=["chist"])
            for s in range(48):
                b, view = load_slab_in(w_ada[l], [(s * 256, 256, 0)])
                for cbi in range(2):
                    cbk = 2 * s + cbi
                    for kt in range(NKT):
                        P.add("pe", lambda e, view=view, cbi=cbi, cbk=cbk, kt=kt: e.matmul(
                            pj[0][:, cbk * 3:cbk * 3 + 3], view[:, kt, cbi * 128:(cbi + 1) * 128], scT[:, kt, :],
                            start=(kt == 0), stop=(kt == NKT - 1)),
                            reads=["wsl%d" % b, "scT"], writes=["pj0"])
            P.add("dve", lambda e: e.tensor_tensor(
                modT[:], pj[0][:, 0:288].rearrange("p (c r) -> p c r", r=3),
                bada[:].unsqueeze(2).broadcast_to([128, 96, 3]), ALU.add),
                reads=["pj0", "bada"], writes=["modT"])
            P.add("dve", lambda e: e.tensor_scalar(onep[:], modT[:, 32:64, :], 1.0, None, ALU.add),
                  reads=["modT"], writes=["onep"])

        def process_block(l, blk):
            kind = blk["kind"]
            NT = blk["NT"]
            nseg = blk["nseg"]
            Ls = blk["Ls"]
            xsrc, xdst = blk["src"], blk["dst"]
            ntt = (NT + 127) // 128
            npt = min(NT, 128)
            if kind == "p":
                chunks = [(0, 64 * c, 64, 0) for c in range(NT // 64)]
                segr = [0]
            else:
                chunks = [(s, 32 * s, 32, 1 + s) for s in range(2)]
                segr = [1, 2]
            first = blk.get("first", False)
            last = blk.get("last", False)

            barrier()
            with contextlib.ExitStack() as bst:
                def BB(name, shape, dt=F32):
                    return bst.enter_context(SBT(name, list(shape), dt))

                mixedT = BB("mixedT", [128, 64, TB], BF16)
                hT = BB("hT", [128, NKT, TB], BF16)

                stage(2)
                with contextlib.ExitStack() as ast:
                    X = ast.enter_context(SBT("XA", [128, D], F32))
                    XN = ast.enter_context(SBT("XNA", [128, D], BF16))
                    stt = ast.enter_context(SBT("sttA", [128, 8, 6], F32))
                    mv = ast.enter_context(SBT("mvA", [128, 2], F32))
                    rs = ast.enter_context(SBT("rsA", [128, 1], F32))
                    nb = ast.enter_context(SBT("nbA", [128, 1], F32))
                    for tt in range(ntt):
                        r0 = tt * 128
                        P.add("sp", lambda e, r0=r0: e.dma_start(out=X[0:npt, :], in_=xsrc[r0:r0 + npt, :]),
                              reads=[blk["srck"]], writes=["XA"], dma=True)
                        for c in range(8):
                            P.add("dve", lambda e, c=c: e.bn_stats(stt[0:npt, c, :], X[0:npt, c * 512:(c + 1) * 512]),
                                  reads=["XA"], writes=["sttA"])
                        P.add("dve", lambda e: e.bn_aggr(mv[0:npt, :], stt[0:npt, :, :]), reads=["sttA"], writes=["mvA"])
                        P.add("act", lambda e: e.activation(out=rs[0:npt, :], in_=mv[0:npt, 1:2], func=AF.Ln, bias=LN_EPS),
                              reads=["mvA"], writes=["rsA"])
                        P.add("act", lambda e: e.activation(out=rs[0:npt, :], in_=rs[0:npt, :], func=AF.Exp, scale=-0.5),
                              reads=["rsA"], writes=["rsA"])
                        P.add("dve", lambda e: e.scalar_tensor_tensor(nb[0:npt, :], mv[0:npt, 0:1], -1.0, rs[0:npt, :],
                                                                      ALU.mult, ALU.mult),
                              reads=["mvA", "rsA"], writes=["nbA"])
                        P.add("act", lambda e: e.activation(out=XN[0:npt, :], in_=X[0:npt, :], func=AF.Identity,
                                                            bias=nb[0:npt, :], scale=rs[0:npt, :]),
                              reads=["XA", "rsA", "nbA"], writes=["XNA"])
                        for q in range(4):
                            for i in range(8):
                                kt = q * 8 + i
                                P.add("pe", lambda e, i=i, kt=kt: e.transpose(
                                    tpx[:, i * 128:i * 128 + npt], XN[0:npt, kt * 128:(kt + 1) * 128], identb[0:npt, 0:npt]),
                                    reads=["XNA", "identb"], writes=["tpx"])
                            for i in range(8):
                                kt = q * 8 + i
                                for si in range(nseg):
                                    r = segr[si]
                                    if kind == "p":
                                        c0, c1 = 0, npt
                                    else:
                                        c0, c1 = si * Ls, (si + 1) * Ls
                                    eng = "dve" if (i % 2 == 0) else "pool_never"
                                    P.add("dve", lambda e, i=i, kt=kt, r=r, c0=c0, c1=c1, r0=r0: e.tensor_scalar(
                                        hT[:, kt, r0 + c0:r0 + c1], tpx[:, i * 128 + c0:i * 128 + c1],
                                        onep[:, kt, r:r + 1], modT[:, kt, r:r + 1], ALU.mult, ALU.add),
                                        reads=["tpx", "onep", "modT"], writes=["hT"])
                barrier()

                def proj_cb(view, b, cbi, evac):
                    i = next_pj()
                    for kt in range(NKT):
                        P.add("pe", lambda e, kt=kt, i=i: e.matmul(
                            pj[i][:, 0:NT], view[:, kt, cbi * 128:(cbi + 1) * 128], hT[:, kt, 0:NT],
                            start=(kt == 0), stop=(kt == NKT - 1)),
                            reads=["wsl%d" % b, "hT"], writes=["pj%d" % i])
                    evac(pj[i], "pj%d" % i)

                def proj_group4(col0, evacs):
                    accs4 = [(pj[0], "pj0"), (pj[1], "pj1"), (bigB[:, 0, :], "bigB0"), (bigB[:, 1, :], "bigB1")]
                    for ks in range(2):
                        b, view = load_slab_in(w_in[l][ks * 2048:(ks + 1) * 2048, :], [(col0, 512, 0)], nkt=16, width=512)
                        for a4 in range(4):
                            acc, acck = accs4[a4]
                            for kt in range(16):
                                kk = ks * 16 + kt
                                P.add("pe", lambda e, acc=acc, kt=kt, kk=kk, a4=a4, view=view: e.matmul(
                                    acc[:, 0:NT], view[:, kt, a4 * 128:(a4 + 1) * 128], hT[:, kk, 0:NT],
                                    start=(kk == 0), stop=(kk == NKT - 1)),
                                    reads=["wsl%d" % b, "hT"], writes=[acck])
                    for a4 in range(4):
                        evacs[a4](*accs4[a4])

                stage(3)
                with contextlib.ExitStack() as cst:
                    def CB_(name, shape, dt=F32):
                        return cst.enter_context(SBT(name, list(shape), dt))
                    dtall = CB_("dtall", [64, 8, 96])
                    aall = CB_("aall", [64, 8, 96])
                    wdt_cm = SBT("wdt", [128, NKT, 96], BF16)
                    wdt = wdt_cm.__enter__()
                    srcdt = (w_fake[0:D, :] if FAKEW else w_in[l]).rearrange("(kt p) c -> p kt c", p=128)
                    dtc0 = 0 if FAKEW else IN_DIM - 96
                    for q in range(4):
                        P.add("pool", lambda e, q=q: e.dma_start(out=wdt[:, q * 8:(q + 1) * 8, :],
                                                                 in_=srcdt[:, q * 8:(q + 1) * 8, dtc0:dtc0 + 96]),
                              writes=["wdt"], dma=True)
                    for ci, (sg, t0, Lc, r) in enumerate(chunks):
                        for kt in range(NKT):
                            P.add("pe", lambda e, kt=kt, t0=t0, Lc=Lc: e.matmul(
                                misc[0:Lc, 0:96], hT[:, kt, t0:t0 + Lc], wdt[:, kt, :],
                                start=(kt == 0), stop=(kt == NKT - 1)),
                                reads=["hT", "wdt"], writes=["misc"])
                        P.add("dve", lambda e, ci=ci, Lc=Lc: e.tensor_tensor(
                            dtall[0:Lc, ci, :], misc[0:Lc, 0:96], dtb[0:Lc, :], ALU.add),
                            reads=["misc", "lp"], writes=["dtall"])
                        P.add("act", lambda e, ci=ci, Lc=Lc: e.activation(out=dtall[0:Lc, ci, :], in_=dtall[0:Lc, ci, :], func=AF.Exp),
                              reads=["dtall"], writes=["dtall"])
                        P.add("act", lambda e, ci=ci, Lc=Lc: e.activation(out=dtall[0:Lc, ci, :], in_=dtall[0:Lc, ci, :], func=AF.Ln, bias=1.0),
                              reads=["dtall"], writes=["dtall"])
                        P.add("dve", lambda e, ci=ci, Lc=Lc: e.tensor_tensor(
                            aall[0:Lc, ci, :], dtall[0:Lc, ci, :], Abc[0:Lc, :], ALU.mult),
                            reads=["dtall", "Abc"], writes=["aall"])

                    barrier()
                    wdt_cm.__exit__(None, None, None)

                    stage(4)
                    with contextlib.ExitStack() as pst:
                        def PB(name, shape, dt=F32):
                            return pst.enter_context(SBT(name, list(shape), dt))
                        SW = nseg * (15 + Ls)
                        uf = PB("uf", [128, SW])
                        ta = PB("pta", [128, SW])
                        tb = PB("ptb", [128, SW])
                        tcn = PB("ptc", [128, 16])
                        wpl = PB("wpl", [128, 4, 512], BF16)
                        pooled = PB("pooled", [128, 4, TB], BF16)
                        gate = PB("gate", [128, 4, TB], BF16)
                        for p in range(4):
                            w = 2 ** (p + 1)
                            evs = []
                            for half in range(2):
                                for cbi in range(2):
                                    j = half * 2 + cbi
                                    ct = p * 4 + j

                                    def evac_u(ps, psk, j=j, ct=ct, w=w, p=p):
                                        for si in range(nseg):
                                            base = si * (15 + Ls)
                                            P.add("act", lambda e, si=si, base=base: e.copy(
                                                uf[:, base + 15:base + 15 + Ls], ps[:, si * Ls:(si + 1) * Ls]),
                                                reads=[psk], writes=["uf"])
                                            P.add("dve", lambda e, si=si, base=base: e.tensor_copy(
                                                uf[:, base:base + 15], uhist[:, ct, si, :]),
                                                reads=["uhist"], writes=["uf"])
                                            P.add("dve", lambda e, si=si, base=base: e.tensor_copy(
                                                uhist[:, ct, si, :], uf[:, base + Ls:base + Ls + 15]),
                                                reads=["uf"], writes=["uhist"])
                                            src, srck = uf, "uf"
                                            tmps = [(ta, "pta"), (tb, "ptb")]
                                            for k in range(p + 1):
                                                dd = 2 ** k
                                                c0 = base + 2 ** (k + 1) - 1
                                                c1 = base + 15 + Ls
                                                dst, dstk = tmps[k % 2]
                                                P.add("dve", lambda e, src=src, dst=dst, c0=c0, c1=c1, dd=dd: e.tensor_tensor(
                                                    dst[:, c0:c1], src[:, c0:c1], src[:, c0 - dd:c1 - dd], ALU.add),
                                                    reads=[srck], writes=[dstk])
                                                src, srck = dst, dstk
                                            P.add("dve", lambda e, src=src, base=base, si=si: e.scalar_tensor_tensor(
                                                pooled[:, j, si * Ls:(si + 1) * Ls], src[:, base + 15:base + 15 + Ls], 1.0 / w,
                                                uf[:, base + 15:base + 15 + Ls], ALU.mult, ALU.subtract),
                                                reads=[srck, "uf"], writes=["pooled"])
                                            if kind == "p" and first:
                                                P.add("dve", lambda e, src=src: e.tensor_tensor(
                                                    tcn[:, :], src[:, 15:31], invc[:, p, :], ALU.mult),
                                                    reads=[srck, "consts"], writes=["ptc"])
                                                P.add("dve", lambda e: e.tensor_tensor(
                                                    pooled[:, j, 0:16], tcn[:, :], uf[:, 15:31], ALU.subtract),
                                                    reads=["ptc", "uf"], writes=["pooled"])
                                    evs.append(evac_u)
                            proj_group4(p * 512, evs)
                            evs = []
                            for half in range(2):
                                for cbi in range(2):
                                    j = half * 2 + cbi

                                    def evac_g(ps, psk, j=j):
                                        P.add("act", lambda e: e.activation(out=gate[:, j, 0:NT], in_=ps[:, 0:NT], func=AF.Silu),
                                              reads=[psk], writes=["gate"])
                                    evs.append(evac_g)
                            proj_group4(W_POOL + p * 512, evs)
                            srcp = w_pool[l, p].rearrange("(kt q) d -> q kt d", q=128)
                            P.add("pool", lambda e, srcp=srcp: e.dma_start(out=wpl[:], in_=srcp), writes=["wpl"], dma=True)
                            for db in range(4):
                                i = next_pj()
                                for kt in range(4):
                                    P.add("pe", lambda e, i=i, kt=kt, db=db: e.matmul(
                                        pj[i][:, 0:NT], wpl[:, kt, db * 128:(db + 1) * 128], pooled[:, kt, 0:NT],
                                        start=(kt == 0), stop=(kt == 3)),
                                        reads=["wpl", "pooled"], writes=["pj%d" % i])
                                P.add("dve", lambda e, i=i, db=db, p=p: e.scalar_tensor_tensor(
                                    mixedT[:, p * 4 + db, 0:NT], pj[i][:, 0:NT], psc[:, p * 4 + db:p * 4 + db + 1],
                                    gate[:, db, 0:NT], ALU.mult, ALU.mult),
                                    reads=["pj%d" % i, "gate", "lp"], writes=["mx%d" % (p * 4 + db)])
                    barrier()

                    stage(5)
                    with contextlib.ExitStack() as sst:
                        def SS(name, shape, dt=F32):
                            return sst.enter_context(SBT(name, list(shape), dt))
                        CW = nseg * (3 + Ls)
                        xc = SS("xc", [128, 6, TB], BF16)
                        Bf = SS("Bf", [128, TB], BF16)
                        Cf = SS("Cf", [128, TB], BF16)
                        xpre = SS("xpre", [128, CW])
                        acc = SS("acc", [128, TB])
                        xs_tm = SS("xs_tm", [64, 768], BF16)
                        xdt = SS("xdt", [64, 768], BF16)
                        xdd = SS("xdd", [64, 768], BF16)
                        B_tm = SS("B_tm", [64, 128], BF16)
                        CBm = SS("CBm", [64, 64])
                        arhs = SS("arhs", [64, 768])
                        E = SS("E", [64, 768], BF16)
                        MT = SS("MT", [64, 768], BF16)
                        y1 = SS("y1", [64, 768])
                        yg = SS("yg", [128, 6, 64])
                        sq = SS("sq", [128, 6, 64], BF16)
                        S = SS("S", [128, 768])
                        S_bf = SS("S_bf", [128, 768], BF16)
                        ea = SS("ea", [64, 12])
                        cd = SS("cd", [128, 12])
                        rstd = SS("rstd", [128, 64])
                        stg = SS("stg", [128, 6, 128])

                        def conv_tile(ps, psk, ct, out_ap_fn, outk):
                            for si in range(nseg):
                                base = si * (3 + Ls)
                                P.add("act", lambda e, si=si, base=base: e.copy(
                                    xpre[:, base + 3:base + 3 + Ls], ps[:, si * Ls:(si + 1) * Ls]),
                                    reads=[psk], writes=["xpre"])
                                P.add("dve", lambda e, si=si, base=base: e.tensor_copy(
                                    xpre[:, base:base + 3], chist[:, ct, si, :]), reads=["chist"], writes=["xpre"])
                                P.add("dve", lambda e, si=si, base=base: e.tensor_copy(
                                    chist[:, ct, si, :], xpre[:, base + Ls:base + Ls + 3]), reads=["xpre"], writes=["chist"])
                                a0 = si * Ls
                                P.add("dve", lambda e, base=base, a0=a0: e.tensor_scalar(
                                    acc[:, a0:a0 + Ls], xpre[:, base:base + Ls], cw[:, ct, 0:1], cb[:, ct:ct + 1], ALU.mult, ALU.add),
                                    reads=["xpre", "lp"], writes=["acc"])
                                for k in range(1, 4):
                                    P.add("dve", lambda e, base=base, a0=a0, k=k: e.scalar_tensor_tensor(
                                        acc[:, a0:a0 + Ls], xpre[:, base + k:base + k + Ls], cw[:, ct, k:k + 1], acc[:, a0:a0 + Ls],
                                        ALU.mult, ALU.add),
                                        reads=["xpre", "lp", "acc"], writes=["acc"])
                            P.add("act", lambda e: e.activation(out=out_ap_fn(), in_=acc[:, 0:NT], func=AF.Silu),
                                  reads=["acc"], writes=[outk])

                        for g in range(8):
                            for s3 in range(3):
                                b, view = load_slab_in(w_in[l], [(2 * W_POOL + 768 * g + 256 * s3, 256, 0)])
                                for cbi in range(2):
                                    j = s3 * 2 + cbi

                                    def evac_z(ps, psk, j=j, g=g):
                                        P.add("act", lambda e: e.activation(out=mixedT[:, 16 + 6 * g + j, 0:NT], in_=ps[:, 0:NT], func=AF.Silu),
                                              reads=[psk], writes=["zs"])
                                    proj_cb(view, b, cbi, evac_z)
                            XB0 = 2 * W_POOL + W_SSD
                            for s3 in range(3):
                                b, view = load_slab_in(w_in[l], [(XB0 + 768 * g + 256 * s3, 256, 0)])
                                for cbi in range(2):
                                    j = s3 * 2 + cbi

                                    def evac_x(ps, psk, j=j):
                                        conv_tile(ps, psk, 6 * g + j, lambda: xc[:, j, 0:NT], "xc")
                                    proj_cb(view, b, cbi, evac_x)
                            b, view = load_slab_in(w_in[l], [(XB0 + W_SSD + 128 * g, 128, 0),
                                                             (XB0 + W_SSD + 1024 + 128 * g, 128, 128)])
                            proj_cb(view, b, 0, lambda ps, psk: conv_tile(ps, psk, 48 + g, lambda: Bf[:, 0:NT], "Bf"))
                            proj_cb(view, b, 1, lambda ps, psk: conv_tile(ps, psk, 56 + g, lambda: Cf[:, 0:NT], "Cf"))

                            g12 = g * 12
                            if kind == "p":
                                if first:
                                    P.add("dve", lambda e: e.memset(S[:], 0.0), writes=["S"])
                                else:
                                    P.add("sp", lambda e, g=g: e.dma_start(out=S[:], in_=stscr[g]),
                                          reads=["stscr%d" % g], writes=["S"], dma=True)
                                P.add("act", lambda e: e.copy(S_bf[:], S[:]), reads=["S"], writes=["S_bf"])

                            for ci, (sg, t0, Lc, r) in enumerate(chunks):
                                cols = slice(t0, t0 + Lc)
                                if kind == "s":
                                    srcS = sssm[l, sg, 768 * g:768 * (g + 1), :].rearrange("(j q) n -> q j n", q=128)
                                    P.add("sp", lambda e, srcS=srcS: e.dma_start(out=stg[:], in_=srcS), writes=["stg"], dma=True)
                                    for j in range(6):
                                        P.add("pe", lambda e, j=j: e.transpose(
                                            bigB[:, j // 3, (j % 3) * 128:(j % 3 + 1) * 128], stg[:, j, :], ident[:, :]),
                                            reads=["stg", "ident"], writes=["bigB"])
                                    P.add("act", lambda e: e.copy(
                                        S[:].rearrange("n (a b) -> n a b", a=2), bigB[:, :, 0:384]),
                                        reads=["bigB"], writes=["S"])
                                    P.add("act", lambda e: e.copy(S_bf[:], S[:]), reads=["S"], writes=["S_bf"])
                                stage(5.1)
                                for j in range(6):
                                    P.add("pe", lambda e, j=j, cols=cols, Lc=Lc: e.transpose(
                                        tpx[0:Lc, j * 128:(j + 1) * 128], xc[:, j, cols], identb[:, :]),
                                        reads=["xc", "identb"], writes=["tpx"])
                                P.add("pe", lambda e, cols=cols, Lc=Lc: e.transpose(
                                    tpx[0:Lc, 768:896], Bf[:, cols], identb[:, :]),
                                    reads=["Bf", "identb"], writes=["tpx"])
                                P.add("act", lambda e, Lc=Lc: e.copy(xs_tm[0:Lc, :], tpx[0:Lc, 0:768]), reads=["tpx"], writes=["xs_tm"])
                                P.add("act", lambda e, Lc=Lc: e.copy(B_tm[0:Lc, :], tpx[0:Lc, 768:896]), reads=["tpx"], writes=["B_tm"])
                                P.add("dve", lambda e, Lc=Lc, ci=ci: e.tensor_tensor(
                                    xdt[0:Lc, :].rearrange("l (h p) -> l h p", h=12),
                                    xs_tm[0:Lc, :].rearrange("l (h p) -> l h p", h=12),
                                    dtall[0:Lc, ci, g12:g12 + 12].unsqueeze(2).broadcast_to([Lc, 12, 64]), ALU.mult),
                                    reads=["xs_tm", "dtall"], writes=["xdt"])
                                stage(5.2)
                                P.add("pe", lambda e, cols=cols, Lc=Lc: e.matmul(
                                    misc[0:Lc, 0:Lc], Bf[:, cols], Cf[:, cols], start=True, stop=True),
                                    reads=["Bf", "Cf"], writes=["misc"])
                                P.add("dve", lambda e, Lc=Lc: e.tensor_tensor(
                                    CBm[0:Lc, 0:Lc], misc[0:Lc, 0:Lc], tle[0:Lc, 0:Lc], ALU.mult),
                                    reads=["misc", "consts"], writes=["CBm"])
                                stage(5.3)
                                P.add("dve", lambda e, Lc=Lc, ci=ci: e.tensor_tensor(
                                    arhs[0:Lc, 0:12 * Lc].rearrange("t (h l) -> t h l", h=12),
                                    aall[0:Lc, ci, g12:g12 + 12].unsqueeze(2).broadcast_to([Lc, 12, Lc]),
                                    tle[0:Lc, 0:Lc].unsqueeze(1).broadcast_to([Lc, 12, Lc]), ALU.mult),
                                    reads=["aall", "consts"], writes=["arhs"])
                                for hf in range(2):
                                    P.add("pe", lambda e, Lc=Lc, hf=hf: e.matmul(
                                        bigA[0:Lc, hf, 0:6 * Lc], ustrict[0:Lc, 0:Lc], arhs[0:Lc, hf * 6 * Lc:(hf + 1) * 6 * Lc],
                                        start=True, stop=True),
                                        reads=["arhs", "consts"], writes=["bigA"])
                                P.add("pe", lambda e, Lc=Lc, ci=ci: e.matmul(
                                    misc[0:Lc, 64:76], tle[0:Lc, 0:Lc], aall[0:Lc, ci, g12:g12 + 12], start=True, stop=True),
                                    reads=["aall", "consts"], writes=["misc"])
                                P.add("pe", lambda e, Lc=Lc, ci=ci: e.matmul(
                                    misc[:, 76:88], ones[0:Lc, :], aall[0:Lc, ci, g12:g12 + 12], start=True, stop=True),
                                    reads=["aall", "ones"], writes=["misc"])
                                stage(5.4)
                                P.add("act", lambda e, Lc=Lc: e.activation(
                                    out=E[0:Lc, 0:12 * Lc].rearrange("t (a b) -> t a b", a=2), in_=bigA[0:Lc, :, 0:6 * Lc], func=AF.Exp),
                                    reads=["bigA"], writes=["E"])
                                P.add("act", lambda e, Lc=Lc: e.activation(out=ea[0:Lc, :], in_=misc[0:Lc, 64:76], func=AF.Exp),
                                      reads=["misc"], writes=["ea"])
                                P.add("act", lambda e: e.activation(out=cd[:, :], in_=misc[:, 76:88], func=AF.Exp),
                                      reads=["misc"], writes=["cd"])
                                P.add("dve", lambda e, Lc=Lc: e.tensor_tensor(
                                    MT[0:Lc, 0:12 * Lc].rearrange("t (h l) -> t h l", h=12),
                                    E[0:Lc, 0:12 * Lc].rearrange("t (h l) -> t h l", h=12),
                                    CBm[0:Lc, 0:Lc].unsqueeze(1).broadcast_to([Lc, 12, Lc]), ALU.mult),
                                    reads=["E", "CBm"], writes=["MT"])
                                P.add("dve", lambda e, Lc=Lc: e.tensor_tensor(
                                    xdd[0:Lc, :].rearrange("l (h p) -> l h p", h=12),
                                    xdt[0:Lc, :].rearrange("l (h p) -> l h p", h=12),
                                    E[0:Lc, 0:12 * Lc].rearrange("t (h l) -> t h l", h=12)[:, :, Lc - 1:Lc].broadcast_to([Lc, 12, 64]),
                                    ALU.mult),
                                    reads=["xdt", "E"], writes=["xdd"])
                                stage(5.5)
                                for hf in range(2):
                                    P.add("pe", lambda e, Lc=Lc, hf=hf, cols=cols: e.matmul(
                                        bigB[0:Lc, hf, 0:384], Cf[:, cols], S_bf[:, hf * 384:(hf + 1) * 384], start=True, stop=True),
                                        reads=["Cf", "S_bf"], writes=["bigB"])
                                for h in range(12):
                                    P.add("pe", lambda e, Lc=Lc, h=h: e.matmul(
                                        bigA[0:Lc, h // 6, (h % 6) * 64:(h % 6 + 1) * 64], MT[0:Lc, h * Lc:(h + 1) * Lc],
                                        xdt[0:Lc, h * 64:(h + 1) * 64], start=True, stop=True),
                                        reads=["MT", "xdt", "E"], writes=["bigA"])
                                y1v = lambda Lc=Lc: y1[0:Lc, :].rearrange("l (a b p) -> l a b p", a=2, b=6)
                                P.add("dve", lambda e, Lc=Lc, y1v=y1v: e.tensor_tensor(
                                    y1v(), bigB[0:Lc, :, 0:384].rearrange("l a (b p) -> l a b p", b=6),
                                    ea[0:Lc, :].rearrange("l (a b) -> l a b", a=2).unsqueeze(3).broadcast_to([Lc, 2, 6, 64]), ALU.mult),
                                    reads=["bigB", "ea"], writes=["y1"])
                                P.add("dve", lambda e, Lc=Lc, y1v=y1v: e.tensor_tensor(
                                    y1v(), y1v(), bigA[0:Lc, :, 0:384].rearrange("l a (b p) -> l a b p", b=6), ALU.add),
                                    reads=["bigA", "y1"], writes=["y1"])
                                P.add("dve", lambda e, Lc=Lc: e.tensor_tensor(
                                    arhs[0:Lc, :].rearrange("l (h p) -> l h p", h=12),
                                    xs_tm[0:Lc, :].rearrange("l (h p) -> l h p", h=12),
                                    dsk[0:Lc, g12:g12 + 12].unsqueeze(2).broadcast_to([Lc, 12, 64]), ALU.mult),
                                    reads=["xs_tm", "lp"], writes=["arhs"])
                                P.add("dve", lambda e, Lc=Lc: e.tensor_tensor(y1[0:Lc, :], y1[0:Lc, :], arhs[0:Lc, :], ALU.add),
                                      reads=["y1", "arhs"], writes=["y1"])
                                stage(5.6)
                                for j in range(6):
                                    P.add("pe", lambda e, Lc=Lc, j=j: e.transpose(
                                        bigA[:, 0, j * Lc:(j + 1) * Lc], y1[0:Lc, j * 128:(j + 1) * 128], ident[0:Lc, 0:Lc]),
                                        reads=["y1", "ident"], writes=["bigA"])
                                P.add("dve", lambda e, Lc=Lc, cols=cols: e.tensor_tensor(
                                    yg[:, :, 0:Lc], bigA[:, 0, 0:6 * Lc].rearrange("p (j l) -> p j l", j=6), mixedT[:, 16 + 6 * g:22 + 6 * g, cols], ALU.mult),
                                    reads=["bigA", "zs"], writes=["yg"])
                                P.add("act", lambda e, Lc=Lc: e.activation(out=sq[:, :, 0:Lc], in_=yg[:, :, 0:Lc], func=AF.Square),
                                      reads=["yg"], writes=["sq"])
                                for j in range(6):
                                    P.add("pe", lambda e, Lc=Lc, j=j: e.matmul(
                                        misc[:, 128:128 + Lc], onesb[:, :], sq[:, j, 0:Lc], start=(j == 0), stop=(j == 5)),
                                        reads=["sq", "onesb"], writes=["misc"])
                                P.add("act", lambda e, Lc=Lc: e.activation(
                                    out=rstd[:, 0:Lc], in_=misc[:, 128:128 + Lc], func=AF.Ln, bias=RMS_EPS, scale=1.0 / 768.0),
                                    reads=["misc"], writes=["rstd"])
                                P.add("act", lambda e, Lc=Lc: e.activation(out=rstd[:, 0:Lc], in_=rstd[:, 0:Lc], func=AF.Exp, scale=-0.5),
                                      reads=["rstd"], writes=["rstd"])
                                P.add("dve", lambda e, Lc=Lc, cols=cols, g=g: e.tensor_tensor(
                                    mixedT[:, 16 + 6 * g:22 + 6 * g, cols], yg[:, :, 0:Lc],
                                    rstd[:, 0:Lc].unsqueeze(1).broadcast_to([128, 6, Lc]), ALU.mult),
                                    reads=["yg", "rstd"], writes=["mxs%d" % g])
                                stage(5.7)
                                for hf in range(2):
                                    P.add("pe", lambda e, Lc=Lc, hf=hf: e.matmul(
                                        bigB[:, hf, 0:384], B_tm[0:Lc, :], xdd[0:Lc, hf * 384:(hf + 1) * 384], start=True, stop=True),
                                        reads=["B_tm", "xdd", "y1"], writes=["bigB"])
                                stage(5.72)
                                P.add("dve", lambda e: e.tensor_tensor(
                                    S[:].rearrange("n (h p) -> n h p", h=12), S[:].rearrange("n (h p) -> n h p", h=12),
                                    cd[:, :].unsqueeze(2).broadcast_to([128, 12, 64]), ALU.mult),
                                    reads=["S", "cd"], writes=["S"])
                                stage(5.74)
                                P.add("dve", lambda e: e.tensor_tensor(
                                    S[:].rearrange("n (a b) -> n a b", a=2), S[:].rearrange("n (a b) -> n a b", a=2),
                                    bigB[:, :, 0:384], ALU.add),
                                    reads=["S", "bigB"], writes=["S"])
                                stage(5.76)
                                P.add("act", lambda e: e.copy(S_bf[:], S[:]), reads=["S"], writes=["S_bf"])
                                stage(5.8)
                                if kind == "s":
                                    store_state(l, g, nssm_s[l, sg], S, stg)
                            if kind == "p":
                                if last:
                                    store_state(l, g, nssm_p[l], S, stg)
                                else:
                                    P.add("sp", lambda e, g=g: e.dma_start(out=stscr[g], in_=S[:]),
                                          reads=["S"], writes=["stscr%d" % g], dma=True)
                            for j in range(6):
                                P.add("dve", lambda e, j=j, g=g: e.tensor_scalar(
                                    mixedT[:, 16 + 6 * g + j, 0:NT], mixedT[:, 16 + 6 * g + j, 0:NT],
                                    nw[:, 6 * g + j:6 * g + j + 1], None, ALU.mult),
                                    reads=["mxs%d" % g, "lp"], writes=["mxs%d" % g])
                    barrier()

                stage(6)
                with contextlib.ExitStack() as cst2:
                    ogs = [cst2.enter_context(SBT("og%d" % i, [128, TB], F32)) for i in range(2)]
                    osts = [cst2.enter_context(SBT("ost%d" % i, [128, 4, 512], F32)) for i in range(2)]
                    mxkeys = ["mx%d" % i for i in range(16)] + ["mxs%d" % i for i in range(8)]
                    odst = oscr.rearrange("(tt p) d -> p tt d", p=128)
                    accs = [(pj[0], "pj0"), (pj[1], "pj1"), (bigB[:, 0, :], "bigB0"), (bigB[:, 1, :], "bigB1")]
                    for cg in range(8):
                        for ks in range(4):
                            b, view = load_slab_in(w_out[l][ks * 2048:(ks + 1) * 2048, :], [(cg * 512, 512, 0)],
                                                   nkt=16, width=512)
                            for a4 in range(4):
                                acc, acck = accs[a4]
                                for kt in range(16):
                                    kk = ks * 16 + kt
                                    P.add("pe", lambda e, acc=acc, kt=kt, kk=kk, a4=a4, view=view: e.matmul(
                                        acc[:, 0:NT], view[:, kt, a4 * 128:(a4 + 1) * 128], mixedT[:, kk, 0:NT],
                                        start=(kk == 0), stop=(kk == 63)),
                                        reads=["wsl%d" % b] + (mxkeys if kk == 0 else []), writes=[acck])
                        ost = osts[cg % 2]
                        ostk = "ost%d" % (cg % 2)
                        for a4 in range(4):
                            acc, acck = accs[a4]
                            db = cg * 4 + a4
                            og = ogs[a4 % 2]
                            ogk = "og%d" % (a4 % 2)
                            bAk = "bigA%d" % (a4 % 2)
                            for si in range(nseg):
                                r = segr[si]
                                c0, c1 = (0, NT) if kind == "p" else (si * Ls, (si + 1) * Ls)
                                P.add("act", lambda e, acc=acc, og=og, c0=c0, c1=c1, r=r, db=db: e.activation(
                                    out=og[:, c0:c1], in_=acc[:, c0:c1], func=AF.Copy, scale=modT[:, 64 + db, r:r + 1]),
                                    reads=[acck, "modT"], writes=[ogk])
                            for tt in range(ntt):
                                P.add("pe", lambda e, tt=tt, og=og, a4=a4: e.transpose(
                                    bigA[0:npt, a4 % 2, tt * 128:(tt + 1) * 128], og[:, tt * 128:tt * 128 + npt], ident[:, :]),
                                    reads=[ogk, "ident"], writes=[bAk])
                            P.add("dve", lambda e, ost=ost, a4=a4: e.tensor_copy(
                                ost[0:npt, 0:ntt, a4 * 128:(a4 + 1) * 128],
                                bigA[0:npt, a4 % 2, 0:ntt * 128].rearrange("p (t d) -> p t d", t=ntt)),
                                reads=[bAk], writes=[ostk])
                        P.add("sp", lambda e, cg=cg, ost=ost: e.dma_start(
                            out=odst[0:npt, 0:ntt, cg * 512:(cg + 1) * 512], in_=ost[0:npt, 0:ntt, :]),
                            reads=[ostk], writes=["oscr"], dma=True)
                barrier()

            stage(7)
            with contextlib.ExitStack() as dst_:
                def DB(name, shape, dt=F32):
                    return dst_.enter_context(SBT(name, list(shape), dt))
                X = DB("XD", [128, D])
                O = DB("OD", [128, D])
                G = DB("GD", [128, D])
                Bb = DB("BD", [128, D])
                stt = DB("sttD", [128, 8, 6])
                mv = DB("mvD", [128, 2])
                rs = DB("rsD", [128, 1])
                nb = DB("nbD", [128, 1])
                P.add("sp", lambda e: e.dma_start(out=G[:], in_=lng_bc[l]), writes=["GD"], dma=True)
                P.add("sp", lambda e: e.dma_start(out=Bb[:], in_=lnb_bc[l]), writes=["BD"], dma=True)
                for tt in range(ntt):
                    r0 = tt * 128
                    P.add("sp", lambda e, r0=r0: e.dma_start(out=O[0:npt, :], in_=oscr[r0:r0 + npt, :]),
                          reads=["oscr"], writes=["OD"], dma=True)
                    P.add("sp", lambda e, r0=r0: e.dma_start(out=X[0:npt, :], in_=xsrc[r0:r0 + npt, :]),
                          reads=[blk["srck"]], writes=["XD"], dma=True)
                    P.add("dve", lambda e: e.scalar_tensor_tensor(X[0:npt, :], X[0:npt, :], ALPHA, O[0:npt, :], ALU.mult, ALU.add),
                          reads=["XD", "OD"], writes=["XD"])
                    for c in range(8):
                        P.add("dve", lambda e, c=c: e.bn_stats(stt[0:npt, c, :], X[0:npt, c * 512:(c + 1) * 512]),
                              reads=["XD"], writes=["sttD"])
                    P.add("dve", lambda e: e.bn_aggr(mv[0:npt, :], stt[0:npt, :, :]), reads=["sttD"], writes=["mvD"])
                    P.add("act", lambda e: e.activation(out=rs[0:npt, :], in_=mv[0:npt, 1:2], func=AF.Ln, bias=LN_EPS),
                          reads=["mvD"], writes=["rsD"])
                    P.add("act", lambda e: e.activation(out=rs[0:npt, :], in_=rs[0:npt, :], func=AF.Exp, scale=-0.5),
                          reads=["rsD"], writes=["rsD"])
                    P.add("dve", lambda e: e.scalar_tensor_tensor(nb[0:npt, :], mv[0:npt, 0:1], -1.0, rs[0:npt, :], ALU.mult, ALU.mult),
                          reads=["mvD", "rsD"], writes=["nbD"])
                    P.add("act", lambda e: e.activation(out=X[0:npt, :], in_=X[0:npt, :], func=AF.Identity,
                                                        bias=nb[0:npt, :], scale=rs[0:npt, :]),
                          reads=["XD", "rsD", "nbD"], writes=["XD"])
                    P.add("dve", lambda e: e.tensor_tensor(X[0:npt, :], X[0:npt, :], G[0:npt, :], ALU.mult),
                          reads=["XD", "GD"], writes=["XD"])
                    P.add("dve", lambda e: e.tensor_tensor(X[0:npt, :], X[0:npt, :], Bb[0:npt, :], ALU.add),
                          reads=["XD", "BD"], writes=["XD"])
                    P.add("sp", lambda e, r0=r0: e.dma_start(out=xdst[r0:r0 + npt, :], in_=X[0:npt, :]),
                          reads=["XD"], writes=[blk["dstk"]], dma=True)

        def store_state(l, g, dstap, S, stg):
            for j in range(6):
                P.add("pe", lambda e, j=j: e.transpose(
                    bigB[:, j // 3, (j % 3) * 128:(j % 3 + 1) * 128], S[:, j * 128:(j + 1) * 128], ident[:, :]),
                    reads=["S", "ident"], writes=["bigB"])
            stage(5.82)
            P.add("act", lambda e: e.copy(stg[:].rearrange("q (a b) n -> q a (b n)", a=2), bigB[:, :, 0:384]),
                  reads=["bigB"], writes=["stg"])
            stage(5.84)
            dd = dstap[768 * g:768 * (g + 1), :].rearrange("(j q) n -> q j n", q=128)
            P.add("sp", lambda e, dd=dd: e.dma_start(out=dd, in_=stg[:]), reads=["stg"], dma=True)
            stage(5.86)

        def store_hist(l, seg, pool_dst, conv_dst):
            stage(8)
            barrier()
            with SBT("hstg", [16, 1024], F32) as hstg:
                for half in range(2):
                    for i in range(8):
                        ct = half * 8 + i
                        P.add("pe", lambda e, i=i, ct=ct: e.transpose(
                            bigA[0:15, i // 4, (i % 4) * 128:(i % 4 + 1) * 128], uhist[:, ct, seg, :], ident[:, :]),
                            reads=["uhist", "ident"], writes=["bigA"])
                    P.add("act", lambda e: e.copy(hstg[0:15, :].rearrange("t (a b) -> t a b", a=2), bigA[0:15, :, :]),
                          reads=["bigA"], writes=["hstg"])
                    P.add("sp", lambda e, half=half: e.dma_start(out=pool_dst[:, half * 1024:(half + 1) * 1024], in_=hstg[0:15, :]),
                          reads=["hstg"], dma=True)
                for pc in range(8):
                    for i in range(8):
                        ct = pc * 8 + i
                        P.add("pe", lambda e, i=i, ct=ct: e.transpose(
                            bigA[0:3, i // 4, (i % 4) * 128:(i % 4 + 1) * 128], chist[:, ct, seg, :], ident[:, :]),
                            reads=["chist", "ident"], writes=["bigA"])
                    P.add("act", lambda e: e.copy(hstg[0:3, :].rearrange("t (a b) -> t a b", a=2), bigA[0:3, :, :]),
                          reads=["bigA"], writes=["hstg"])
                    P.add("sp", lambda e, pc=pc: e.dma_start(out=conv_dst[:, pc * 1024:(pc + 1) * 1024], in_=hstg[0:3, :]),
                          reads=["hstg"], dma=True)
            barrier()

        def load_hist_sample(l):
            barrier()
            with SBT("hstg2", [16, 1024], F32) as hstg:
                for seg in range(2):
                    for half in range(2):
                        P.add("sp", lambda e, seg=seg, half=half: e.dma_start(
                            out=hstg[0:15, :], in_=spool[l, seg, :, half * 1024:(half + 1) * 1024]), writes=["hstg"], dma=True)
                        for i in range(8):
                            P.add("pe", lambda e, i=i: e.transpose(
                                misc[:, i * 15:(i + 1) * 15], hstg[0:15, i * 128:(i + 1) * 128], ident[0:15, 0:15]),
                                reads=["hstg", "ident"], writes=["misc"])
                        P.add("act", lambda e, seg=seg, half=half: e.copy(
                            uhist[:, half * 8:(half + 1) * 8, seg, :], misc[:, 0:120].rearrange("p (i t) -> p i t", i=8)),
                            reads=["misc"], writes=["uhist"])
                    for pc in range(8):
                        P.add("sp", lambda e, seg=seg, pc=pc: e.dma_start(
                            out=hstg[0:3, :], in_=sconv[l, seg, :, pc * 1024:(pc + 1) * 1024]), writes=["hstg"], dma=True)
                        for i in range(8):
                            P.add("pe", lambda e, i=i: e.transpose(
                                misc[:, i * 3:(i + 1) * 3], hstg[0:3, i * 128:(i + 1) * 128], ident[0:3, 0:3]),
                                reads=["hstg", "ident"], writes=["misc"])
                        P.add("act", lambda e, seg=seg, pc=pc: e.copy(
                            chist[:, pc * 8:(pc + 1) * 8, seg, :], misc[:, 0:24].rearrange("p (i t) -> p i t", i=8)),
                            reads=["misc"], writes=["chist"])
            barrier()

        setup()
        for l in LAYERS:
            lfirst, llast = (l == LAYERS[0]), (l == LAYERS[-1])
            layer_setup(l)
            for bi in range(NPBLK):
                rows = slice(bi * TB, (bi + 1) * TB)
                blk = dict(kind="p", NT=TB, nseg=1, Ls=TB, first=(bi == 0), last=(bi == NPBLK - 1),
                           src=(xp if lfirst else x1p)[rows, :], dst=(yp if llast else x1p)[rows, :],
                           srck=("xp" if lfirst else "x1p%d" % bi), dstk=("yp%d" % bi if llast else "x1p%d" % bi))
                process_block(l, blk)
            if NPBLK:
                store_hist(l, 0, npool_p[l], nconv_p[l])
            if DO_SAMPLE:
                load_hist_sample(l)
                blk = dict(kind="s", NT=64, nseg=2, Ls=32,
                           src=(xs if lfirst else x1s), dst=(ys if llast else x1s),
                           srck=("xs" if lfirst else "x1s"), dstk=("ys" if llast else "x1s"))
                process_block(l, blk)
                for seg in range(2):
                    store_hist(l, seg, npool_s[l, seg], nconv_s[l, seg])
        P.emit()
    return nc


def _fm(v, ntile):
    sh = v.shape[:-1]
    return np.ascontiguousarray(np.moveaxis(v.reshape(sh + (ntile, 128)), -1, -2))


def make_in_maps(inp, n_cores=8):
    f = lambda a: np.ascontiguousarray(np.asarray(a, dtype=np.float32))
    x_prompt, x_sample = f(inp["x_prompt"]), f(inp["x_sample"])
    c_prompt, c_sample = f(inp["c_prompt"]), f(inp["c_sample"])
    state_pool, state_conv, state_ssm = f(inp["state_pool"]), f(inp["state_conv"]), f(inp["state_ssm"])
    w_ada, w_in, w_pool, w_out = f(inp["w_ada"]), f(inp["w_in"]), f(inp["w_pool"]), f(inp["w_out"])
    shared = {
        "w_ada": w_ada, "w_in": w_in, "w_pool": w_pool, "w_out": w_out,
        "badaT": _fm(f(inp["b_ada"]), 96),
        "pscT": _fm(f(inp["pool_scale"]), 16),
        "cwT": np.ascontiguousarray(np.transpose(f(inp["conv_w"]).reshape(2, 4, 64, 128), (0, 3, 2, 1))),
        "cbT": _fm(f(inp["conv_b"]), 64),
        "nwT": _fm(f(inp["ssd_norm_w"]), 48),
        "h96": np.ascontiguousarray(np.broadcast_to(
            np.stack([f(inp["dt_bias"]), f(inp["a_log"]), f(inp["d_skip"])], axis=1)[:, :, None, :], (2, 3, 128, 96))),
        "lng_bc": np.ascontiguousarray(np.broadcast_to(f(inp["ln_g"])[:, None, :], (2, 128, D))),
        "lnb_bc": np.ascontiguousarray(np.broadcast_to(f(inp["ln_b"])[:, None, :], (2, 128, D))),
        "c_ident": np.eye(128, dtype=np.float32),
        "c_ustrict": np.ascontiguousarray((np.arange(64)[:, None] > np.arange(64)[None, :]).astype(np.float32)),
        "c_tle": np.ascontiguousarray((np.arange(64)[:, None] <= np.arange(64)[None, :]).astype(np.float32)),
        "c_invc": np.ascontiguousarray(np.broadcast_to(
            np.stack([1.0 / np.minimum(np.arange(16) + 1, w) for w in (2, 4, 8, 16)]).astype(np.float32)[None], (128, 4, 16))),
    }
    maps = []
    for c in range(n_cores):
        pc = c % 4
        crow = np.stack([c_prompt[pc], c_sample[2 * c], c_sample[2 * c + 1]], axis=0)
        m = dict(shared)
        m["xp"] = x_prompt[pc]
        m["xs"] = np.ascontiguousarray(x_sample[2 * c:2 * c + 2].reshape(64, D))
        m["cT"] = np.ascontiguousarray(np.transpose(crow.reshape(3, NKT, 128), (2, 1, 0)))
        m["spool"] = np.ascontiguousarray(state_pool[:, 2 * c:2 * c + 2])
        m["sconv"] = np.ascontiguousarray(state_conv[:, 2 * c:2 * c + 2])
        m["sssm"] = np.ascontiguousarray(state_ssm[:, 2 * c:2 * c + 2].reshape(2, 2, W_SSD, 128))
        maps.append(m)
    return maps


def assemble(results):
    yp = np.stack([results[c]["yp"] for c in range(4)], axis=0)
    ys = np.concatenate([results[c]["ys"].reshape(2, 32, D) for c in range(8)], axis=0)
    npp = np.stack([results[c]["npool_p"] for c in range(4)], axis=1)
    ncp = np.stack([results[c]["nconv_p"] for c in range(4)], axis=1)
    nsp = np.stack([results[c]["nssm_p"].reshape(2, 96, 64, 128) for c in range(4)], axis=1)
    nps = np.concatenate([results[c]["npool_s"] for c in range(8)], axis=1)
    ncs = np.concatenate([results[c]["nconv_s"] for c in range(8)], axis=1)
    nss = np.concatenate([results[c]["nssm_s"].reshape(2, 2, 96, 64, 128) for c in range(8)], axis=1)
    return tuple(np.ascontiguousarray(a, dtype=np.float32) for a in (yp, ys, npp, ncp, nsp, nps, ncs, nss))


def kernel(**inputs):
    nc = build()
    in_maps = make_in_maps(inputs)
    res = run_bass_kernel_spmd(nc, in_maps, core_ids=list(range(8)))
    return assemble(res.results)
```

```python
import contextlib
import types
import numpy as np
import concourse.bass as bass
import concourse.mybir as mybir
from concourse.bass_utils import run_bass_kernel_spmd

F32 = mybir.dt.float32
BF16 = mybir.dt.bfloat16
AF = mybir.ActivationFunctionType
ALU = mybir.AluOpType

ENGS = ("pe", "act", "dve", "pool", "sp")
NDSEM = 8

D = 4096
NKT = 32
IN_DIM = 18528
W_POOL = 2048
W_SSD = 6144
CONV_DIM = 8192
ALPHA = 4.0 ** 0.25
LN_EPS = 1e-5
RMS_EPS = 1e-5
SEQ = 2048
TB = 512
NPB = SEQ // TB


class Op:
    __slots__ = ("eng", "fn", "dma", "deps", "signals", "count", "dsem", "dtarget", "idx", "qidx")

    def __init__(self, eng, fn, dma):
        self.eng = eng
        self.fn = fn
        self.dma = dma
        self.deps = set()
        self.signals = False
        self.count = 0
        self.dsem = None
        self.dtarget = 0
        self.idx = -1
        self.qidx = -1


def _freeze(fn):
    if not fn.__closure__:
        return fn
    cells = []
    for c in fn.__closure__:
        try:
            cells.append(types.CellType(c.cell_contents))
        except ValueError:
            cells.append(c)
    return types.FunctionType(fn.__code__, fn.__globals__, fn.__name__, fn.__defaults__, tuple(cells))


class Prog:
    def __init__(self, nc, same_engine_sync=True):
        self.nc = nc
        self.ops = []
        self.res = {}
        self.ndma = {e: 0 for e in ENGS}
        self.dma_ops = {e: [] for e in ENGS}
        self.same_engine_sync = same_engine_sync
        self.enabled = True
        self.maxops = 10 ** 9

    def add(self, eng, fn, reads=(), writes=(), dma=False, ph=True):
        if not self.enabled or len(self.ops) >= self.maxops:
            return None
        op = Op(eng, _freeze(fn), dma)
        op.idx = len(self.ops)
        reads = list(reads)
        if ph:
            reads.append("PH")
        for k in reads:
            r = self.res.setdefault(k, [None, []])
            if r[0] is not None:
                op.deps.add(r[0])
        for k in writes:
            r = self.res.setdefault(k, [None, []])
            if r[0] is not None:
                op.deps.add(r[0])
            op.deps.update(r[1])
        for k in reads:
            self.res[k][1].append(op.idx)
        for k in writes:
            self.res[k] = [op.idx, []]
        if dma:
            j = self.ndma[eng]
            op.qidx = j
            self.ndma[eng] += 1
            if j >= NDSEM:
                op.deps.add(self.dma_ops[eng][j - NDSEM].idx)
            self.dma_ops[eng].append(op)
        op.deps.discard(op.idx)
        self.ops.append(op)
        return op

    def emit(self):
        nc = self.nc
        ops = self.ops
        ses = self.same_engine_sync

        def skip(p, op):
            return (not p.dma) and p.eng == op.eng and (p.eng == "pe" or not ses) and not op.dma

        for op in ops:
            latest = {}
            keep = set()
            for d in op.deps:
                p = ops[d]
                if p.dma:
                    keep.add(d)
                elif latest.get(p.eng, -1) < d:
                    latest[p.eng] = d
            keep.update(latest.values())
            op.deps = keep
        for op in ops:
            for d in op.deps:
                p = ops[d]
                if p.dma or skip(p, op):
                    continue
                p.signals = True
        cnt = {e: 0 for e in ENGS}
        for op in ops:
            if not op.dma and op.signals:
                cnt[op.eng] += 1
                op.count = cnt[op.eng]
        with contextlib.ExitStack() as st:
            esem = {e: st.enter_context(nc.semaphore("s_" + e)) for e in ENGS}
            dsem = {}
            for e in ENGS:
                if self.ndma[e]:
                    dsem[e] = [st.enter_context(nc.semaphore("d_%s%d" % (e, i))) for i in range(NDSEM)]
            for e in ENGS:
                for op in self.dma_ops[e]:
                    op.dsem = dsem[e][op.qidx % NDSEM]
                    op.dtarget = 16 * (op.qidx // NDSEM + 1)
            block = st.enter_context(nc.Block())
            byeng = {e: [op for op in ops if op.eng == e] for e in ENGS}

            def run(e, eng):
                waited = {}
                for op in byeng[e]:
                    need = {}
                    for d in op.deps:
                        p = ops[d]
                        if p.dma:
                            key = ("d", p.eng, p.qidx % NDSEM)
                            sem, val = p.dsem, p.dtarget
                        else:
                            if skip(p, op):
                                continue
                            key = ("e", p.eng)
                            sem, val = esem[p.eng], p.count
                        if need.get(key, (None, 0))[1] < val:
                            need[key] = (sem, val)
                    for key, (sem, val) in need.items():
                        if waited.get(key, 0) >= val:
                            continue
                        waited[key] = val
                        eng.wait_ge(sem, val)
                    ins = op.fn(eng)
                    if op.dma:
                        ins.then_inc(op.dsem, 16)
                    elif op.signals:
                        ins.then_inc(esem[e], 1)
                if e == "sp":
                    for q in ENGS:
                        n = self.ndma[q]
                        for i in range(min(n, NDSEM)):
                            k = (n - i + NDSEM - 1) // NDSEM
                            eng.wait_ge(dsem[q][i], 16 * k)

            @block.tensor
            def _(eng):
                run("pe", eng)

            @block.scalar
            def _(eng):
                run("act", eng)

            @block.vector
            def _(eng):
                run("dve", eng)

            @block.gpsimd
            def _(eng):
                run("pool", eng)

            @block.sync
            def _(eng):
                run("sp", eng)


def build(cfg=None):
    cfg = cfg or {}
    LAYERS = cfg.get("layers", [0, 1])
    NPBLK = cfg.get("npblk", NPB)
    DO_SAMPLE = cfg.get("sample", True)

    nc = bass.Bass("TRN2", target_bir_lowering=False)

    FAKEW = cfg.get("fakew", False)

    def din(name, shape):
        if FAKEW and name in ("w_ada", "w_in", "w_out"):
            return None
        return nc.dram_tensor(name, list(shape), F32, kind="ExternalInput").ap()

    def dout(name, shape):
        return nc.dram_tensor(name, list(shape), F32, kind="ExternalOutput").ap()

    xp = din("xp", [SEQ, D])
    xs = din("xs", [64, D])
    cT = din("cT", [128, NKT, 3])
    spool = din("spool", [2, 2, 15, W_POOL])
    sconv = din("sconv", [2, 2, 3, CONV_DIM])
    sssm = din("sssm", [2, 2, W_SSD, 128])
    w_ada = din("w_ada", [2, D, 3 * D])
    badaT = din("badaT", [2, 128, 96])
    w_in = din("w_in", [2, D, IN_DIM])
    w_pool = din("w_pool", [2, 4, 512, 512])
    w_out = din("w_out", [2, 2 * D, D])
    pscT = din("pscT", [2, 128, 16])
    cwT = din("cwT", [2, 128, 64, 4])
    cbT = din("cbT", [2, 128, 64])
    nwT = din("nwT", [2, 128, 48])
    h96 = din("h96", [2, 3, 128, 96])
    lng_bc = din("lng_bc", [2, 128, D])
    lnb_bc = din("lnb_bc", [2, 128, D])
    c_ident = din("c_ident", [128, 128])
    c_ustrict = din("c_ustrict", [64, 64])
    c_tle = din("c_tle", [64, 64])
    c_invc = din("c_invc", [128, 4, 16])

    w_fake = nc.dram_tensor("w_fake", [8192, 512], F32, kind="ExternalInput").ap() if FAKEW else None
    if FAKEW:
        w_ada = w_in = w_out = [w_fake, w_fake]

    yp = dout("yp", [SEQ, D])
    ys = dout("ys", [64, D])
    npool_p = dout("npool_p", [2, 15, W_POOL])
    nconv_p = dout("nconv_p", [2, 3, CONV_DIM])
    nssm_p = dout("nssm_p", [2, W_SSD, 128])
    npool_s = dout("npool_s", [2, 2, 15, W_POOL])
    nconv_s = dout("nconv_s", [2, 2, 3, CONV_DIM])
    nssm_s = dout("nssm_s", [2, 2, W_SSD, 128])

    x1p = nc.dram_tensor("x1p", [SEQ, D], F32).ap()
    x1s = nc.dram_tensor("x1s", [64, D], F32).ap()
    oscr = nc.dram_tensor("oscr", [TB, D], F32).ap()
    stscr = nc.dram_tensor("stscr", [8, 128, 768], F32).ap()

    _uid = [0]

    def SBT(name, shape, dt=F32):
        _uid[0] += 1
        return nc.sbuf_tensor("%s_%d" % (name, _uid[0]), list(shape), dt)

    with contextlib.ExitStack() as st:
        def SB(name, shape, dt=F32):
            return st.enter_context(SBT(name, list(shape), dt))

        def PS(name, shape, dt=F32):
            return st.enter_context(nc.psum_tensor(name, list(shape), dt))

        wsl = [SB("wsl%d" % i, [128, 8192], BF16) for i in range(2)]
        ident = SB("ident", [128, 128])
        identb = SB("identb", [128, 128], BF16)
        ustrict = SB("ustrict", [64, 64])
        tle = SB("tle", [64, 64])
        ones = SB("ones", [128, 128])
        onesb = SB("onesb", [128, 128], BF16)
        invc = SB("invc", [128, 4, 16])
        cTs = SB("cTs", [128, NKT, 3])
        scT = SB("scT", [128, NKT, 3], BF16)
        modT = SB("modT", [128, 96, 3])
        onep = SB("onep", [128, 32, 3])
        bada = SB("bada", [128, 96])
        psc = SB("psc", [128, 16])
        cw = SB("cw", [128, 64, 4])
        cb = SB("cb", [128, 64])
        nw = SB("nw", [128, 48])
        dtb = SB("dtb", [128, 96])
        Abc = SB("Abc", [128, 96])
        dsk = SB("dsk", [128, 96])
        uhist = SB("uhist", [128, 16, 2, 15])
        chist = SB("chist", [128, 64, 2, 3])
        scr1 = SB("scr1", [128, 8])

        pj = [PS("pj%d" % i, [128, 512]) for i in range(2)]
        misc = PS("misc", [128, 512])
        tpx = PS("tpx", [128, 1024], BF16)
        bigA = PS("bigA", [128, 2, 512])
        bigB = PS("bigB", [128, 2, 512])

        P = Prog(nc)
        P.maxops = cfg.get("maxops", 10 ** 9)
        state = {"slab": 0, "pj": 0, "ev": 0}

        STAGE = cfg.get("stage", 99)

        def stage(k):
            if k > STAGE:
                P.enabled = False

        def barrier():
            P.add("dve", lambda e: e.memset(scr1[:, 0:1], 0.0), writes=["PH"], ph=False)

        def load_slab_in(wsrc, pieces, nkt=NKT, width=256):
            b = state["slab"] % 2
            state["slab"] += 1
            view = wsl[b][:, 0:nkt * width].rearrange("p (k c) -> p k c", k=nkt)
            if FAKEW:
                wsrc = w_fake[0:nkt * 128, :]
                pieces = [(0, ncol, dc) for (c0, ncol, dc) in pieces]
            src = wsrc.rearrange("(kt p) c -> p kt c", p=128)
            for (c0, ncol, dc) in pieces:
                for q in range(nkt // 8):
                    P.add("pool", lambda e, c0=c0, ncol=ncol, dc=dc, q=q, view=view, src=src: e.dma_start(
                        out=view[:, q * 8:(q + 1) * 8, dc:dc + ncol], in_=src[:, q * 8:(q + 1) * 8, c0:c0 + ncol]),
                        writes=["wsl%d" % b], dma=True, ph=False)
            return b, view

        def next_pj():
            i = state["pj"] % 2
            state["pj"] += 1
            return i

        def setup():
            P.add("sp", lambda e: e.dma_start(out=ident[:], in_=c_ident[:, :]), writes=["ident"], dma=True)
            P.add("sp", lambda e: e.dma_start(out=ustrict[:], in_=c_ustrict[:, :]), writes=["consts"], dma=True)
            P.add("sp", lambda e: e.dma_start(out=tle[:], in_=c_tle[:, :]), writes=["consts"], dma=True)
            P.add("sp", lambda e: e.dma_start(out=invc[:], in_=c_invc[:, :, :]), writes=["consts"], dma=True)
            P.add("sp", lambda e: e.dma_start(out=cTs[:], in_=cT[:, :, :]), writes=["cTs"], dma=True)
            P.add("dve", lambda e: e.tensor_copy(identb[:], ident[:]), reads=["ident"], writes=["identb"])
            P.add("dve", lambda e: e.memset(ones[:], 1.0), writes=["ones"])
            P.add("dve", lambda e: e.tensor_copy(onesb[:], ones[:]), reads=["ones"], writes=["onesb"])
            P.add("act", lambda e: e.activation(out=scT[:], in_=cTs[:], func=AF.Silu), reads=["cTs"], writes=["scT"])

        def layer_setup(l):
            stage(1)
            barrier()
            P.add("sp", lambda e: e.dma_start(out=bada[:], in_=badaT[l]), writes=["bada"], dma=True)
            P.add("sp", lambda e: e.dma_start(out=psc[:], in_=pscT[l]), writes=["lp"], dma=True)
            P.add("sp", lambda e: e.dma_start(out=cw[:], in_=cwT[l]), writes=["lp"], dma=True)
            P.add("sp", lambda e: e.dma_start(out=cb[:], in_=cbT[l]), writes=["lp"], dma=True)
            P.add("sp", lambda e: e.dma_start(out=nw[:], in_=nwT[l]), writes=["lp"], dma=True)
            P.add("sp", lambda e: e.dma_start(out=dtb[:], in_=h96[l, 0]), writes=["lp"], dma=True)
            P.add("sp", lambda e: e.dma_start(out=Abc[:], in_=h96[l, 1]), writes=["Abc"], dma=True)
            P.add("sp", lambda e: e.dma_start(out=dsk[:], in_=h96[l, 2]), writes=["lp"], dma=True)
            P.add("act", lambda e: e.activation(out=Abc[:], in_=Abc[:], func=AF.Exp), reads=["Abc"], writes=["Abc"])
            P.add("dve", lambda e: e.tensor_scalar(Abc[:], Abc[:], -1.0, None, ALU.mult), reads=["Abc"], writes=["Abc"])
            P.add("dve", lambda e: e.memset(uhist[:], 0.0), writes=["uhist"])
            P.add("dve", lambda e: e.memset(chist[:], 0.0), writes=["chist"])
            accs_ada = [(pj[0], "pj0"), (pj[1], "pj1"), (bigB[:, 0, :], "bigB0"), (bigB[:, 1, :], "bigB1")]
            for s in range(24):
                for ks in range(2):
                    b, view = load_slab_in(w_ada[l][ks * 2048:(ks + 1) * 2048, :], [(s * 512, 512, 0)],
                                           nkt=16, width=512)
                    for a4 in range(4):
                        acc, acck = accs_ada[a4]
                        for kt in range(16):
                            kk = ks * 16 + kt
                            P.add("pe", lambda e, acc=acc, view=view, a4=a4, s=s, kt=kt, kk=kk: e.matmul(
                                acc[:, s * 3:s * 3 + 3], view[:, kt, a4 * 128:(a4 + 1) * 128], scT[:, kk, :],
                                start=(kk == 0), stop=(kk == NKT - 1)),
                                reads=["wsl%d" % b, "scT"], writes=[acck])
                for a4 in range(4):
                    acc, acck = accs_ada[a4]
                    cbk = 4 * s + a4
                    P.add("dve", lambda e, acc=acc, s=s, cbk=cbk: e.tensor_scalar(
                        modT[:, cbk, :], acc[:, s * 3:s * 3 + 3], bada[:, cbk:cbk + 1], None, ALU.add),
                        reads=[acck, "bada"], writes=["modT"])
            P.add("dve", lambda e: e.tensor_scalar(onep[:], modT[:, 32:64, :], 1.0, None, ALU.add),
                  reads=["modT"], writes=["onep"])

        def process_block(l, blk):
            kind = blk["kind"]
            NT = blk["NT"]
            nseg = blk["nseg"]
            Ls = blk["Ls"]
            xsrc, xdst = blk["src"], blk["dst"]
            ntt = (NT + 127) // 128
            npt = min(NT, 128)
            if kind == "p":
                chunks = [(0, 64 * c, 64, 0) for c in range(NT // 64)]
                segr = [0]
            else:
                chunks = [(s, 32 * s, 32, 1 + s) for s in range(2)]
                segr = [1, 2]
            first = blk.get("first", False)
            last = blk.get("last", False)

            barrier()
            with contextlib.ExitStack() as bst:
                def BB(name, shape, dt=F32):
                    return bst.enter_context(SBT(name, list(shape), dt))

                mixedT = BB("mixedT", [128, 64, TB], BF16)
                hT = BB("hT", [128, NKT, TB], BF16)

                stage(2)
                with contextlib.ExitStack() as ast:
                    X = ast.enter_context(SBT("XA", [128, D], F32))
                    XN = ast.enter_context(SBT("XNA", [128, D], BF16))
                    stt = ast.enter_context(SBT("sttA", [128, 8, 6], F32))
                    mv = ast.enter_context(SBT("mvA", [128, 2], F32))
                    rs = ast.enter_context(SBT("rsA", [128, 1], F32))
                    nb = ast.enter_context(SBT("nbA", [128, 1], F32))
                    for tt in range(ntt):
                        r0 = tt * 128
                        P.add("sp", lambda e, r0=r0: e.dma_start(out=X[0:npt, :], in_=xsrc[r0:r0 + npt, :]),
                              reads=[blk["srck"]], writes=["XA"], dma=True)
                        for c in range(8):
                            P.add("dve", lambda e, c=c: e.bn_stats(stt[0:npt, c, :], X[0:npt, c * 512:(c + 1) * 512]),
                                  reads=["XA"], writes=["sttA"])
                        P.add("dve", lambda e: e.bn_aggr(mv[0:npt, :], stt[0:npt, :, :]), reads=["sttA"], writes=["mvA"])
                        P.add("act", lambda e: e.activation(out=rs[0:npt, :], in_=mv[0:npt, 1:2], func=AF.Ln, bias=LN_EPS),
                              reads=["mvA"], writes=["rsA"])
                        P.add("act", lambda e: e.activation(out=rs[0:npt, :], in_=rs[0:npt, :], func=AF.Exp, scale=-0.5),
                              reads=["rsA"], writes=["rsA"])
                        P.add("dve", lambda e: e.scalar_tensor_tensor(nb[0:npt, :], mv[0:npt, 0:1], -1.0, rs[0:npt, :],
                                                                      ALU.mult, ALU.mult),
                              reads=["mvA", "rsA"], writes=["nbA"])
                        P.add("act", lambda e: e.activation(out=XN[0:npt, :], in_=X[0:npt, :], func=AF.Identity,
                                                            bias=nb[0:npt, :], scale=rs[0:npt, :]),
                              reads=["XA", "rsA", "nbA"], writes=["XNA"])
                        for q in range(4):
                            for i in range(8):
                                kt = q * 8 + i
                                P.add("pe", lambda e, i=i, kt=kt: e.transpose(
                                    tpx[:, i * 128:i * 128 + npt], XN[0:npt, kt * 128:(kt + 1) * 128], identb[0:npt, 0:npt]),
                                    reads=["XNA", "identb"], writes=["tpx"])
                            for i in range(8):
                                kt = q * 8 + i
                                for si in range(nseg):
                                    r = segr[si]
                                    if kind == "p":
                                        c0, c1 = 0, npt
                                    else:
                                        c0, c1 = si * Ls, (si + 1) * Ls
                                    eng = "dve" if (i % 2 == 0) else "pool_never"
                                    P.add("dve", lambda e, i=i, kt=kt, r=r, c0=c0, c1=c1, r0=r0: e.tensor_scalar(
                                        hT[:, kt, r0 + c0:r0 + c1], tpx[:, i * 128 + c0:i * 128 + c1],
                                        onep[:, kt, r:r + 1], modT[:, kt, r:r + 1], ALU.mult, ALU.add),
                                        reads=["tpx", "onep", "modT"], writes=["hT"])
                barrier()

                def proj_cb(view, b, cbi, evac):
                    i = next_pj()
                    for kt in range(NKT):
                        P.add("pe", lambda e, kt=kt, i=i: e.matmul(
                            pj[i][:, 0:NT], view[:, kt, cbi * 128:(cbi + 1) * 128], hT[:, kt, 0:NT],
                            start=(kt == 0), stop=(kt == NKT - 1)),
                            reads=["wsl%d" % b, "hT"], writes=["pj%d" % i])
                    evac(pj[i], "pj%d" % i)

                def proj_group4(col0, evacs):
                    accs4 = [(pj[0], "pj0"), (pj[1], "pj1"), (bigB[:, 0, :], "bigB0"), (bigB[:, 1, :], "bigB1")]
                    for ks in range(2):
                        b, view = load_slab_in(w_in[l][ks * 2048:(ks + 1) * 2048, :], [(col0, 512, 0)], nkt=16, width=512)
                        for a4 in range(4):
                            acc, acck = accs4[a4]
                            for kt in range(16):
                                kk = ks * 16 + kt
                                P.add("pe", lambda e, acc=acc, kt=kt, kk=kk, a4=a4, view=view: e.matmul(
                                    acc[:, 0:NT], view[:, kt, a4 * 128:(a4 + 1) * 128], hT[:, kk, 0:NT],
                                    start=(kk == 0), stop=(kk == NKT - 1)),
                                    reads=["wsl%d" % b, "hT"], writes=[acck])
                    for a4 in range(4):
                        evacs[a4](*accs4[a4])

                stage(3)
                with contextlib.ExitStack() as cst:
                    def CB_(name, shape, dt=F32):
                        return cst.enter_context(SBT(name, list(shape), dt))
                    dtall = CB_("dtall", [64, 8, 96])
                    aall = CB_("aall", [64, 8, 96])
                    wdt_cm = SBT("wdt", [128, NKT, 96], BF16)
                    wdt = wdt_cm.__enter__()
                    srcdt = (w_fake[0:D, :] if FAKEW else w_in[l]).rearrange("(kt p) c -> p kt c", p=128)
                    dtc0 = 0 if FAKEW else IN_DIM - 96
                    for q in range(4):
                        P.add("pool", lambda e, q=q: e.dma_start(out=wdt[:, q * 8:(q + 1) * 8, :],
                                                                 in_=srcdt[:, q * 8:(q + 1) * 8, dtc0:dtc0 + 96]),
                              writes=["wdt"], dma=True)
                    for ci, (sg, t0, Lc, r) in enumerate(chunks):
                        for kt in range(NKT):
                            P.add("pe", lambda e, kt=kt, t0=t0, Lc=Lc: e.matmul(
                                misc[0:Lc, 0:96], hT[:, kt, t0:t0 + Lc], wdt[:, kt, :],
                                start=(kt == 0), stop=(kt == NKT - 1)),
                                reads=["hT", "wdt"], writes=["misc"])
                        P.add("dve", lambda e, ci=ci, Lc=Lc: e.tensor_tensor(
                            dtall[0:Lc, ci, :], misc[0:Lc, 0:96], dtb[0:Lc, :], ALU.add),
                            reads=["misc", "lp"], writes=["dtall"])
                        P.add("act", lambda e, ci=ci, Lc=Lc: e.activation(out=dtall[0:Lc, ci, :], in_=dtall[0:Lc, ci, :], func=AF.Exp),
                              reads=["dtall"], writes=["dtall"])
                        P.add("act", lambda e, ci=ci, Lc=Lc: e.activation(out=dtall[0:Lc, ci, :], in_=dtall[0:Lc, ci, :], func=AF.Ln, bias=1.0),
                              reads=["dtall"], writes=["dtall"])
                        P.add("dve", lambda e, ci=ci, Lc=Lc: e.tensor_tensor(
                            aall[0:Lc, ci, :], dtall[0:Lc, ci, :], Abc[0:Lc, :], ALU.mult),
                            reads=["dtall", "Abc"], writes=["aall"])

                    barrier()
                    wdt_cm.__exit__(None, None, None)

                    stage(4)
                    with contextlib.ExitStack() as pst:
                        def PB(name, shape, dt=F32):
                            return pst.enter_context(SBT(name, list(shape), dt))
                        SW = nseg * (15 + Ls)
                        uf = PB("uf", [128, SW])
                        ta = PB("pta", [128, SW])
                        tb = PB("ptb", [128, SW])
                        tcn = PB("ptc", [128, 16])
                        wpl = PB("wpl", [128, 4, 512], BF16)
                        pooled = PB("pooled", [128, 4, TB], BF16)
                        gate = PB("gate", [128, 4, TB], BF16)
                        for p in range(4):
                            w = 2 ** (p + 1)
                            evs = []
                            for half in range(2):
                                for cbi in range(2):
                                    j = half * 2 + cbi
                                    ct = p * 4 + j

                                    def evac_u(ps, psk, j=j, ct=ct, w=w, p=p):
                                        for si in range(nseg):
                                            base = si * (15 + Ls)
                                            P.add("act", lambda e, si=si, base=base: e.copy(
                                                uf[:, base + 15:base + 15 + Ls], ps[:, si * Ls:(si + 1) * Ls]),
                                                reads=[psk], writes=["uf"])
                                            P.add("dve", lambda e, si=si, base=base: e.tensor_copy(
                                                uf[:, base:base + 15], uhist[:, ct, si, :]),
                                                reads=["uhist"], writes=["uf"])
                                            P.add("dve", lambda e, si=si, base=base: e.tensor_copy(
                                                uhist[:, ct, si, :], uf[:, base + Ls:base + Ls + 15]),
                                                reads=["uf"], writes=["uhist"])
                                            src, srck = uf, "uf"
                                            tmps = [(ta, "pta"), (tb, "ptb")]
                                            for k in range(p + 1):
                                                dd = 2 ** k
                                                c0 = base + 2 ** (k + 1) - 1
                                                c1 = base + 15 + Ls
                                                dst, dstk = tmps[k % 2]
                                                P.add("dve", lambda e, src=src, dst=dst, c0=c0, c1=c1, dd=dd: e.tensor_tensor(
                                                    dst[:, c0:c1], src[:, c0:c1], src[:, c0 - dd:c1 - dd], ALU.add),
                                                    reads=[srck], writes=[dstk])
                                                src, srck = dst, dstk
                                            P.add("dve", lambda e, src=src, base=base, si=si: e.scalar_tensor_tensor(
                                                pooled[:, j, si * Ls:(si + 1) * Ls], src[:, base + 15:base + 15 + Ls], 1.0 / w,
                                                uf[:, base + 15:base + 15 + Ls], ALU.mult, ALU.subtract),
                                                reads=[srck, "uf"], writes=["pooled"])
                                            if kind == "p" and first:
                                                P.add("dve", lambda e, src=src: e.tensor_tensor(
                                                    tcn[:, :], src[:, 15:31], invc[:, p, :], ALU.mult),
                                                    reads=[srck, "consts"], writes=["ptc"])
                                                P.add("dve", lambda e: e.tensor_tensor(
                                                    pooled[:, j, 0:16], tcn[:, :], uf[:, 15:31], ALU.subtract),
                                                    reads=["ptc", "uf"], writes=["pooled"])
                                    evs.append(evac_u)
                            proj_group4(p * 512, evs)
                            evs = []
                            for half in range(2):
                                for cbi in range(2):
                                    j = half * 2 + cbi

                                    def evac_g(ps, psk, j=j):
                                        P.add("act", lambda e: e.activation(out=gate[:, j, 0:NT], in_=ps[:, 0:NT], func=AF.Silu),
                                              reads=[psk], writes=["gate"])
                                    evs.append(evac_g)
                            proj_group4(W_POOL + p * 512, evs)
                            srcp = w_pool[l, p].rearrange("(kt q) d -> q kt d", q=128)
                            P.add("pool", lambda e, srcp=srcp: e.dma_start(out=wpl[:], in_=srcp), writes=["wpl"], dma=True)
                            for db in range(4):
                                i = next_pj()
                                for kt in range(4):
                                    P.add("pe", lambda e, i=i, kt=kt, db=db: e.matmul(
                                        pj[i][:, 0:NT], wpl[:, kt, db * 128:(db + 1) * 128], pooled[:, kt, 0:NT],
                                        start=(kt == 0), stop=(kt == 3)),
                                        reads=["wpl", "pooled"], writes=["pj%d" % i])
                                P.add("dve", lambda e, i=i, db=db, p=p: e.scalar_tensor_tensor(
                                    mixedT[:, p * 4 + db, 0:NT], pj[i][:, 0:NT], psc[:, p * 4 + db:p * 4 + db + 1],
                                    gate[:, db, 0:NT], ALU.mult, ALU.mult),
                                    reads=["pj%d" % i, "gate", "lp"], writes=["mx%d" % (p * 4 + db)])
                    barrier()

                    stage(5)
                    with contextlib.ExitStack() as sst:
                        def SS(name, shape, dt=F32):
                            return sst.enter_context(SBT(name, list(shape), dt))
                        CW = nseg * (3 + Ls)
                        xc = SS("xc", [128, 6, TB], BF16)
                        Bf = SS("Bf", [128, TB], BF16)
                        Cf = SS("Cf", [128, TB], BF16)
                        xpre = SS("xpre", [128, CW])
                        acc = SS("acc", [128, TB])
                        xs_tm = SS("xs_tm", [64, 768], BF16)
                        xdt = SS("xdt", [64, 768], BF16)
                        xdd = SS("xdd", [64, 768], BF16)
                        B_tm = SS("B_tm", [64, 128], BF16)
                        CBm = SS("CBm", [64, 64])
                        arhs = SS("arhs", [64, 768])
                        E = SS("E", [64, 768], BF16)
                        MT = SS("MT", [64, 768], BF16)
                        y1 = SS("y1", [64, 768])
                        yg = SS("yg", [128, 6, 64])
                        sq = SS("sq", [128, 6, 64], BF16)
                        S = SS("S", [128, 768])
                        S_bf = SS("S_bf", [128, 768], BF16)
                        ea = SS("ea", [64, 12])
                        cd = SS("cd", [128, 12])
                        rstd = SS("rstd", [128, 64])
                        stg = SS("stg", [128, 6, 128])

                        def conv_tile(ps, psk, ct, out_ap_fn, outk):
                            for si in range(nseg):
                                base = si * (3 + Ls)
                                P.add("act", lambda e, si=si, base=base: e.copy(
                                    xpre[:, base + 3:base + 3 + Ls], ps[:, si * Ls:(si + 1) * Ls]),
                                    reads=[psk], writes=["xpre"])
                                P.add("dve", lambda e, si=si, base=base: e.tensor_copy(
                                    xpre[:, base:base + 3], chist[:, ct, si, :]), reads=["chist"], writes=["xpre"])
                                P.add("dve", lambda e, si=si, base=base: e.tensor_copy(
                                    chist[:, ct, si, :], xpre[:, base + Ls:base + Ls + 3]), reads=["xpre"], writes=["chist"])
                                a0 = si * Ls
                                P.add("dve", lambda e, base=base, a0=a0: e.tensor_scalar(
                                    acc[:, a0:a0 + Ls], xpre[:, base:base + Ls], cw[:, ct, 0:1], cb[:, ct:ct + 1], ALU.mult, ALU.add),
                                    reads=["xpre", "lp"], writes=["acc"])
                                for k in range(1, 4):
                                    P.add("dve", lambda e, base=base, a0=a0, k=k: e.scalar_tensor_tensor(
                                        acc[:, a0:a0 + Ls], xpre[:, base + k:base + k + Ls], cw[:, ct, k:k + 1], acc[:, a0:a0 + Ls],
                                        ALU.mult, ALU.add),
                                        reads=["xpre", "lp", "acc"], writes=["acc"])
                            P.add("act", lambda e: e.activation(out=out_ap_fn(), in_=acc[:, 0:NT], func=AF.Silu),
                                  reads=["acc"], writes=[outk])

                        for g in range(8):
                            for s3 in range(3):
                                b, view = load_slab_in(w_in[l], [(2 * W_POOL + 768 * g + 256 * s3, 256, 0)])
                                for cbi in range(2):
                                    j = s3 * 2 + cbi

                                    def evac_z(ps, psk, j=j, g=g):
                                        P.add("act", lambda e: e.activation(out=mixedT[:, 16 + 6 * g + j, 0:NT], in_=ps[:, 0:NT], func=AF.Silu),
                                              reads=[psk], writes=["zs"])
                                    proj_cb(view, b, cbi, evac_z)
                            XB0 = 2 * W_POOL + W_SSD
                            for s3 in range(3):
                                b, view = load_slab_in(w_in[l], [(XB0 + 768 * g + 256 * s3, 256, 0)])
                                for cbi in range(2):
                                    j = s3 * 2 + cbi

                                    def evac_x(ps, psk, j=j):
                                        conv_tile(ps, psk, 6 * g + j, lambda: xc[:, j, 0:NT], "xc")
                                    proj_cb(view, b, cbi, evac_x)
                            b, view = load_slab_in(w_in[l], [(XB0 + W_SSD + 128 * g, 128, 0),
                                                             (XB0 + W_SSD + 1024 + 128 * g, 128, 128)])
                            proj_cb(view, b, 0, lambda ps, psk: conv_tile(ps, psk, 48 + g, lambda: Bf[:, 0:NT], "Bf"))
                            proj_cb(view, b, 1, lambda ps, psk: conv_tile(ps, psk, 56 + g, lambda: Cf[:, 0:NT], "Cf"))

                            g12 = g * 12
                            if kind == "p":
                                if first:
                                    P.add("dve", lambda e: e.memset(S[:], 0.0), writes=["S"])
                                else:
                                    P.add("sp", lambda e, g=g: e.dma_start(out=S[:], in_=stscr[g]),
                                          reads=["stscr%d" % g], writes=["S"], dma=True)
                                P.add("act", lambda e: e.copy(S_bf[:], S[:]), reads=["S"], writes=["S_bf"])

                            for ci, (sg, t0, Lc, r) in enumerate(chunks):
                                cols = slice(t0, t0 + Lc)
                                if kind == "s":
                                    srcS = sssm[l, sg, 768 * g:768 * (g + 1), :].rearrange("(j q) n -> q j n", q=128)
                                    P.add("sp", lambda e, srcS=srcS: e.dma_start(out=stg[:], in_=srcS), writes=["stg"], dma=True)
                                    for j in range(6):
                                        P.add("pe", lambda e, j=j: e.transpose(
                                            bigB[:, j // 3, (j % 3) * 128:(j % 3 + 1) * 128], stg[:, j, :], ident[:, :]),
                                            reads=["stg", "ident"], writes=["bigB"])
                                    P.add("act", lambda e: e.copy(
                                        S[:].rearrange("n (a b) -> n a b", a=2), bigB[:, :, 0:384]),
                                        reads=["bigB"], writes=["S"])
                                    P.add("act", lambda e: e.copy(S_bf[:], S[:]), reads=["S"], writes=["S_bf"])
                                stage(5.1)
                                for j in range(6):
                                    P.add("pe", lambda e, j=j, cols=cols, Lc=Lc: e.transpose(
                                        tpx[0:Lc, j * 128:(j + 1) * 128], xc[:, j, cols], identb[:, :]),
                                        reads=["xc", "identb"], writes=["tpx"])
                                P.add("pe", lambda e, cols=cols, Lc=Lc: e.transpose(
                                    tpx[0:Lc, 768:896], Bf[:, cols], identb[:, :]),
                                    reads=["Bf", "identb"], writes=["tpx"])
                                P.add("act", lambda e, Lc=Lc: e.copy(xs_tm[0:Lc, :], tpx[0:Lc, 0:768]), reads=["tpx"], writes=["xs_tm"])
                                P.add("act", lambda e, Lc=Lc: e.copy(B_tm[0:Lc, :], tpx[0:Lc, 768:896]), reads=["tpx"], writes=["B_tm"])
                                P.add("dve", lambda e, Lc=Lc, ci=ci: e.tensor_tensor(
                                    xdt[0:Lc, :].rearrange("l (h p) -> l h p", h=12),
                                    xs_tm[0:Lc, :].rearrange("l (h p) -> l h p", h=12),
                                    dtall[0:Lc, ci, g12:g12 + 12].unsqueeze(2).broadcast_to([Lc, 12, 64]), ALU.mult),
                                    reads=["xs_tm", "dtall"], writes=["xdt"])
                                stage(5.2)
                                P.add("pe", lambda e, cols=cols, Lc=Lc: e.matmul(
                                    misc[0:Lc, 0:Lc], Bf[:, cols], Cf[:, cols], start=True, stop=True),
                                    reads=["Bf", "Cf"], writes=["misc"])
                                P.add("dve", lambda e, Lc=Lc: e.tensor_tensor(
                                    CBm[0:Lc, 0:Lc], misc[0:Lc, 0:Lc], tle[0:Lc, 0:Lc], ALU.mult),
                                    reads=["misc", "consts"], writes=["CBm"])
                                stage(5.3)
                                P.add("dve", lambda e, Lc=Lc, ci=ci: e.tensor_tensor(
                                    arhs[0:Lc, 0:12 * Lc].rearrange("t (h l) -> t h l", h=12),
                                    aall[0:Lc, ci, g12:g12 + 12].unsqueeze(2).broadcast_to([Lc, 12, Lc]),
                                    tle[0:Lc, 0:Lc].unsqueeze(1).broadcast_to([Lc, 12, Lc]), ALU.mult),
                                    reads=["aall", "consts"], writes=["arhs"])
                                for hf in range(2):
                                    P.add("pe", lambda e, Lc=Lc, hf=hf: e.matmul(
                                        bigA[0:Lc, hf, 0:6 * Lc], ustrict[0:Lc, 0:Lc], arhs[0:Lc, hf * 6 * Lc:(hf + 1) * 6 * Lc],
                                        start=True, stop=True),
                                        reads=["arhs", "consts"], writes=["bigA"])
                                P.add("pe", lambda e, Lc=Lc, ci=ci: e.matmul(
                                    misc[0:Lc, 64:76], tle[0:Lc, 0:Lc], aall[0:Lc, ci, g12:g12 + 12], start=True, stop=True),
                                    reads=["aall", "consts"], writes=["misc"])
                                P.add("pe", lambda e, Lc=Lc, ci=ci: e.matmul(
                                    misc[:, 76:88], ones[0:Lc, :], aall[0:Lc, ci, g12:g12 + 12], start=True, stop=True),
                                    reads=["aall", "ones"], writes=["misc"])
                                stage(5.4)
                                P.add("act", lambda e, Lc=Lc: e.activation(
                                    out=E[0:Lc, 0:12 * Lc].rearrange("t (a b) -> t a b", a=2), in_=bigA[0:Lc, :, 0:6 * Lc], func=AF.Exp),
                                    reads=["bigA"], writes=["E"])
                                P.add("act", lambda e, Lc=Lc: e.activation(out=ea[0:Lc, :], in_=misc[0:Lc, 64:76], func=AF.Exp),
                                      reads=["misc"], writes=["ea"])
                                P.add("act", lambda e: e.activation(out=cd[:, :], in_=misc[:, 76:88], func=AF.Exp),
                                      reads=["misc"], writes=["cd"])
                                P.add("dve", lambda e, Lc=Lc: e.tensor_tensor(
                                    MT[0:Lc, 0:12 * Lc].rearrange("t (h l) -> t h l", h=12),
                                    E[0:Lc, 0:12 * Lc].rearrange("t (h l) -> t h l", h=12),
                                    CBm[0:Lc, 0:Lc].unsqueeze(1).broadcast_to([Lc, 12, Lc]), ALU.mult),
                                    reads=["E", "CBm"], writes=["MT"])
                                P.add("dve", lambda e, Lc=Lc: e.tensor_tensor(
                                    xdd[0:Lc, :].rearrange("l (h p) -> l h p", h=12),
                                    xdt[0:Lc, :].rearrange("l (h p) -> l h p", h=12),
                                    E[0:Lc, 0:12 * Lc].rearrange("t (h l) -> t h l", h=12)[:, :, Lc - 1:Lc].broadcast_to([Lc, 12, 64]),
                                    ALU.mult),
                                    reads=["xdt", "E"], writes=["xdd"])
                                stage(5.5)
                                for hf in range(2):
                                    P.add("pe", lambda e, Lc=Lc, hf=hf, cols=cols: e.matmul(
                                        bigB[0:Lc, hf, 0:384], Cf[:, cols], S_bf[:, hf * 384:(hf + 1) * 384], start=True, stop=True),
                                        reads=["Cf", "S_bf"], writes=["bigB"])
                                for h in range(12):
                                    P.add("pe", lambda e, Lc=Lc, h=h: e.matmul(
                                        bigA[0:Lc, h // 6, (h % 6) * 64:(h % 6 + 1) * 64], MT[0:Lc, h * Lc:(h + 1) * Lc],
                                        xdt[0:Lc, h * 64:(h + 1) * 64], start=True, stop=True),
                                        reads=["MT", "xdt", "E"], writes=["bigA"])
                                y1v = lambda Lc=Lc: y1[0:Lc, :].rearrange("l (a b p) -> l a b p", a=2, b=6)
                                P.add("dve", lambda e, Lc=Lc, y1v=y1v: e.tensor_tensor(
                                    y1v(), bigB[0:Lc, :, 0:384].rearrange("l a (b p) -> l a b p", b=6),
                                    ea[0:Lc, :].rearrange("l (a b) -> l a b", a=2).unsqueeze(3).broadcast_to([Lc, 2, 6, 64]), ALU.mult),
                                    reads=["bigB", "ea"], writes=["y1"])
                                P.add("dve", lambda e, Lc=Lc, y1v=y1v: e.tensor_tensor(
                                    y1v(), y1v(), bigA[0:Lc, :, 0:384].rearrange("l a (b p) -> l a b p", b=6), ALU.add),
                                    reads=["bigA", "y1"], writes=["y1"])
                                P.add("dve", lambda e, Lc=Lc: e.tensor_tensor(
                                    arhs[0:Lc, :].rearrange("l (h p) -> l h p", h=12),
                                    xs_tm[0:Lc, :].rearrange("l (h p) -> l h p", h=12),
                                    dsk[0:Lc, g12:g12 + 12].unsqueeze(2).broadcast_to([Lc, 12, 64]), ALU.mult),
                                    reads=["xs_tm", "lp"], writes=["arhs"])
                                P.add("dve", lambda e, Lc=Lc: e.tensor_tensor(y1[0:Lc, :], y1[0:Lc, :], arhs[0:Lc, :], ALU.add),
                                      reads=["y1", "arhs"], writes=["y1"])
                                stage(5.6)
                                for j in range(6):
                                    P.add("pe", lambda e, Lc=Lc, j=j: e.transpose(
                                        bigA[:, 0, j * Lc:(j + 1) * Lc], y1[0:Lc, j * 128:(j + 1) * 128], ident[0:Lc, 0:Lc]),
                                        reads=["y1", "ident"], writes=["bigA"])
                                P.add("dve", lambda e, Lc=Lc, cols=cols: e.tensor_tensor(
                                    yg[:, :, 0:Lc], bigA[:, 0, 0:6 * Lc].rearrange("p (j l) -> p j l", j=6), mixedT[:, 16 + 6 * g:22 + 6 * g, cols], ALU.mult),
                                    reads=["bigA", "zs"], writes=["yg"])
                                P.add("act", lambda e, Lc=Lc: e.activation(out=sq[:, :, 0:Lc], in_=yg[:, :, 0:Lc], func=AF.Square),
                                      reads=["yg"], writes=["sq"])
                                for j in range(6):
                                    P.add("pe", lambda e, Lc=Lc, j=j: e.matmul(
                                        misc[:, 128:128 + Lc], onesb[:, :], sq[:, j, 0:Lc], start=(j == 0), stop=(j == 5)),
                                        reads=["sq", "onesb"], writes=["misc"])
                                P.add("act", lambda e, Lc=Lc: e.activation(
                                    out=rstd[:, 0:Lc], in_=misc[:, 128:128 + Lc], func=AF.Ln, bias=RMS_EPS, scale=1.0 / 768.0),
                                    reads=["misc"], writes=["rstd"])
                                P.add("act", lambda e, Lc=Lc: e.activation(out=rstd[:, 0:Lc], in_=rstd[:, 0:Lc], func=AF.Exp, scale=-0.5),
                                      reads=["rstd"], writes=["rstd"])
                                P.add("dve", lambda e, Lc=Lc, cols=cols, g=g: e.tensor_tensor(
                                    mixedT[:, 16 + 6 * g:22 + 6 * g, cols], yg[:, :, 0:Lc],
                                    rstd[:, 0:Lc].unsqueeze(1).broadcast_to([128, 6, Lc]), ALU.mult),
                                    reads=["yg", "rstd"], writes=["mxs%d" % g])
                                stage(5.7)
                                for hf in range(2):
                                    P.add("pe", lambda e, Lc=Lc, hf=hf: e.matmul(
                                        bigB[:, hf, 0:384], B_tm[0:Lc, :], xdd[0:Lc, hf * 384:(hf + 1) * 384], start=True, stop=True),
                                        reads=["B_tm", "xdd", "y1"], writes=["bigB"])
                                stage(5.72)
                                P.add("dve", lambda e: e.tensor_tensor(
                                    S[:].rearrange("n (h p) -> n h p", h=12), S[:].rearrange("n (h p) -> n h p", h=12),
                                    cd[:, :].unsqueeze(2).broadcast_to([128, 12, 64]), ALU.mult),
                                    reads=["S", "cd"], writes=["S"])
                                stage(5.74)
                                P.add("dve", lambda e: e.tensor_tensor(
                                    S[:].rearrange("n (a b) -> n a b", a=2), S[:].rearrange("n (a b) -> n a b", a=2),
                                    bigB[:, :, 0:384], ALU.add),
                                    reads=["S", "bigB"], writes=["S"])
                                stage(5.76)
                                P.add("act", lambda e: e.copy(S_bf[:], S[:]), reads=["S"], writes=["S_bf"])
                                stage(5.8)
                                if kind == "s":
                                    store_state(l, g, nssm_s[l, sg], S, stg)
                            if kind == "p":
                                if last:
                                    store_state(l, g, nssm_p[l], S, stg)
                                else:
                                    P.add("sp", lambda e, g=g: e.dma_start(out=stscr[g], in_=S[:]),
                                          reads=["S"], writes=["stscr%d" % g], dma=True)
                            for j in range(6):
                                P.add("dve", lambda e, j=j, g=g: e.tensor_scalar(
                                    mixedT[:, 16 + 6 * g + j, 0:NT], mixedT[:, 16 + 6 * g + j, 0:NT],
                                    nw[:, 6 * g + j:6 * g + j + 1], None, ALU.mult),
                                    reads=["mxs%d" % g, "lp"], writes=["mxs%d" % g])
                    barrier()

                stage(6)
                with contextlib.ExitStack() as cst2:
                    ogs = [cst2.enter_context(SBT("og%d" % i, [128, TB], F32)) for i in range(2)]
                    osts = [cst2.enter_context(SBT("ost%d" % i, [128, 4, 512], F32)) for i in range(2)]
                    mxkeys = ["mx%d" % i for i in range(16)] + ["mxs%d" % i for i in range(8)]
                    odst = oscr.rearrange("(tt p) d -> p tt d", p=128)
                    accs = [(pj[0], "pj0"), (pj[1], "pj1"), (bigB[:, 0, :], "bigB0"), (bigB[:, 1, :], "bigB1")]
                    for cg in range(8):
                        for ks in range(4):
                            b, view = load_slab_in(w_out[l][ks * 2048:(ks + 1) * 2048, :], [(cg * 512, 512, 0)],
                                                   nkt=16, width=512)
                            for a4 in range(4):
                                acc, acck = accs[a4]
                                for kt in range(16):
                                    kk = ks * 16 + kt
                                    P.add("pe", lambda e, acc=acc, kt=kt, kk=kk, a4=a4, view=view: e.matmul(
                                        acc[:, 0:NT], view[:, kt, a4 * 128:(a4 + 1) * 128], mixedT[:, kk, 0:NT],
                                        start=(kk == 0), stop=(kk == 63)),
                                        reads=["wsl%d" % b] + (mxkeys if kk == 0 else []), writes=[acck])
                        ost = osts[cg % 2]
                        ostk = "ost%d" % (cg % 2)
                        for a4 in range(4):
                            acc, acck = accs[a4]
                            db = cg * 4 + a4
                            og = ogs[a4 % 2]
                            ogk = "og%d" % (a4 % 2)
                            bAk = "bigA%d" % (a4 % 2)
                            for si in range(nseg):
                                r = segr[si]
                                c0, c1 = (0, NT) if kind == "p" else (si * Ls, (si + 1) * Ls)
                                P.add("act", lambda e, acc=acc, og=og, c0=c0, c1=c1, r=r, db=db: e.activation(
                                    out=og[:, c0:c1], in_=acc[:, c0:c1], func=AF.Copy, scale=modT[:, 64 + db, r:r + 1]),
                                    reads=[acck, "modT"], writes=[ogk])
                            for tt in range(ntt):
                                P.add("pe", lambda e, tt=tt, og=og, a4=a4: e.transpose(
                                    bigA[0:npt, a4 % 2, tt * 128:(tt + 1) * 128], og[:, tt * 128:tt * 128 + npt], ident[:, :]),
                                    reads=[ogk, "ident"], writes=[bAk])
                            P.add("dve", lambda e, ost=ost, a4=a4: e.tensor_copy(
                                ost[0:npt, 0:ntt, a4 * 128:(a4 + 1) * 128],
                                bigA[0:npt, a4 % 2, 0:ntt * 128].rearrange("p (t d) -> p t d", t=ntt)),
                                reads=[bAk], writes=[ostk])
                        P.add("sp", lambda e, cg=cg, ost=ost: e.dma_start(
                            out=odst[0:npt, 0:ntt, cg * 512:(cg + 1) * 512], in_=ost[0:npt, 0:ntt, :]),
                            reads=[ostk], writes=["oscr"], dma=True)
                barrier()

            stage(7)
            with contextlib.ExitStack() as dst_:
                def DB(name, shape, dt=F32):
                    return dst_.enter_context(SBT(name, list(shape), dt))
                X = DB("XD", [128, D])
                O = DB("OD", [128, D])
                G = DB("GD", [128, D])
                Bb = DB("BD", [128, D])
                stt = DB("sttD", [128, 8, 6])
                mv = DB("mvD", [128, 2])
                rs = DB("rsD", [128, 1])
                nb = DB("nbD", [128, 1])
                P.add("sp", lambda e: e.dma_start(out=G[:], in_=lng_bc[l]), writes=["GD"], dma=True)
                P.add("sp", lambda e: e.dma_start(out=Bb[:], in_=lnb_bc[l]), writes=["BD"], dma=True)
                for tt in range(ntt):
                    r0 = tt * 128
                    P.add("sp", lambda e, r0=r0: e.dma_start(out=O[0:npt, :], in_=oscr[r0:r0 + npt, :]),
                          reads=["oscr"], writes=["OD"], dma=True)
                    P.add("sp", lambda e, r0=r0: e.dma_start(out=X[0:npt, :], in_=xsrc[r0:r0 + npt, :]),
                          reads=[blk["srck"]], writes=["XD"], dma=True)
                    P.add("dve", lambda e: e.scalar_tensor_tensor(X[0:npt, :], X[0:npt, :], ALPHA, O[0:npt, :], ALU.mult, ALU.add),
                          reads=["XD", "OD"], writes=["XD"])
                    for c in range(8):
                        P.add("dve", lambda e, c=c: e.bn_stats(stt[0:npt, c, :], X[0:npt, c * 512:(c + 1) * 512]),
                              reads=["XD"], writes=["sttD"])
                    P.add("dve", lambda e: e.bn_aggr(mv[0:npt, :], stt[0:npt, :, :]), reads=["sttD"], writes=["mvD"])
                    P.add("act", lambda e: e.activation(out=rs[0:npt, :], in_=mv[0:npt, 1:2], func=AF.Ln, bias=LN_EPS),
                          reads=["mvD"], writes=["rsD"])
                    P.add("act", lambda e: e.activation(out=rs[0:npt, :], in_=rs[0:npt, :], func=AF.Exp, scale=-0.5),
                          reads=["rsD"], writes=["rsD"])
                    P.add("dve", lambda e: e.scalar_tensor_tensor(nb[0:npt, :], mv[0:npt, 0:1], -1.0, rs[0:npt, :], ALU.mult, ALU.mult),
                          reads=["mvD", "rsD"], writes=["nbD"])
                    P.add("act", lambda e: e.activation(out=X[0:npt, :], in_=X[0:npt, :], func=AF.Identity,
                                                        bias=nb[0:npt, :], scale=rs[0:npt, :]),
                          reads=["XD", "rsD", "nbD"], writes=["XD"])
                    P.add("dve", lambda e: e.tensor_tensor(X[0:npt, :], X[0:npt, :], G[0:npt, :], ALU.mult),
                          reads=["XD", "GD"], writes=["XD"])
                    P.add("dve", lambda e: e.tensor_tensor(X[0:npt, :], X[0:npt, :], Bb[0:npt, :], ALU.add),
                          reads=["XD", "BD"], writes=["XD"])
                    P.add("sp", lambda e, r0=r0: e.dma_start(out=xdst[r0:r0 + npt, :], in_=X[0:npt, :]),
                          reads=["XD"], writes=[blk["dstk"]], dma=True)

        def store_state(l, g, dstap, S, stg):
            for j in range(6):
                P.add("pe", lambda e, j=j: e.transpose(
                    bigB[:, j // 3, (j % 3) * 128:(j % 3 + 1) * 128], S[:, j * 128:(j + 1) * 128], ident[:, :]),
                    reads=["S", "ident"], writes=["bigB"])
            stage(5.82)
            P.add("act", lambda e: e.copy(stg[:].rearrange("q (a b) n -> q a (b n)", a=2), bigB[:, :, 0:384]),
                  reads=["bigB"], writes=["stg"])
            stage(5.84)
            dd = dstap[768 * g:768 * (g + 1), :].rearrange("(j q) n -> q j n", q=128)
            P.add("sp", lambda e, dd=dd: e.dma_start(out=dd, in_=stg[:]), reads=["stg"], dma=True)
            stage(5.86)

        def store_hist(l, seg, pool_dst, conv_dst):
            stage(8)
            barrier()
            with SBT("hstg", [16, 1024], F32) as hstg:
                for half in range(2):
                    for i in range(8):
                        ct = half * 8 + i
                        P.add("pe", lambda e, i=i, ct=ct: e.transpose(
                            bigA[0:15, i // 4, (i % 4) * 128:(i % 4 + 1) * 128], uhist[:, ct, seg, :], ident[:, :]),
                            reads=["uhist", "ident"], writes=["bigA"])
                    P.add("act", lambda e: e.copy(hstg[0:15, :].rearrange("t (a b) -> t a b", a=2), bigA[0:15, :, :]),
                          reads=["bigA"], writes=["hstg"])
                    P.add("sp", lambda e, half=half: e.dma_start(out=pool_dst[:, half * 1024:(half + 1) * 1024], in_=hstg[0:15, :]),
                          reads=["hstg"], dma=True)
                for pc in range(8):
                    for i in range(8):
                        ct = pc * 8 + i
                        P.add("pe", lambda e, i=i, ct=ct: e.transpose(
                            bigA[0:3, i // 4, (i % 4) * 128:(i % 4 + 1) * 128], chist[:, ct, seg, :], ident[:, :]),
                            reads=["chist", "ident"], writes=["bigA"])
                    P.add("act", lambda e: e.copy(hstg[0:3, :].rearrange("t (a b) -> t a b", a=2), bigA[0:3, :, :]),
                          reads=["bigA"], writes=["hstg"])
                    P.add("sp", lambda e, pc=pc: e.dma_start(out=conv_dst[:, pc * 1024:(pc + 1) * 1024], in_=hstg[0:3, :]),
                          reads=["hstg"], dma=True)
            barrier()

        def load_hist_sample(l):
            barrier()
            with SBT("hstg2", [16, 1024], F32) as hstg:
                for seg in range(2):
                    for half in range(2):
                        P.add("sp", lambda e, seg=seg, half=half: e.dma_start(
                            out=hstg[0:15, :], in_=spool[l, seg, :, half * 1024:(half + 1) * 1024]), writes=["hstg"], dma=True)
                        for i in range(8):
                            P.add("pe", lambda e, i=i: e.transpose(
                                misc[:, i * 15:(i + 1) * 15], hstg[0:15, i * 128:(i + 1) * 128], ident[0:15, 0:15]),
                                reads=["hstg", "ident"], writes=["misc"])
                        P.add("act", lambda e, seg=seg, half=half: e.copy(
                            uhist[:, half * 8:(half + 1) * 8, seg, :], misc[:, 0:120].rearrange("p (i t) -> p i t", i=8)),
                            reads=["misc"], writes=["uhist"])
                    for pc in range(8):
                        P.add("sp", lambda e, seg=seg, pc=pc: e.dma_start(
                            out=hstg[0:3, :], in_=sconv[l, seg, :, pc * 1024:(pc + 1) * 1024]), writes=["hstg"], dma=True)
                        for i in range(8):
                            P.add("pe", lambda e, i=i: e.transpose(
                                misc[:, i * 3:(i + 1) * 3], hstg[0:3, i * 128:(i + 1) * 128], ident[0:3, 0:3]),
                                reads=["hstg", "ident"], writes=["misc"])
                        P.add("act", lambda e, seg=seg, pc=pc: e.copy(
                            chist[:, pc * 8:(pc + 1) * 8, seg, :], misc[:, 0:24].rearrange("p (i t) -> p i t", i=8)),
                            reads=["misc"], writes=["chist"])
            barrier()

        setup()
        for l in LAYERS:
            lfirst, llast = (l == LAYERS[0]), (l == LAYERS[-1])
            layer_setup(l)
            for bi in range(NPBLK):
                rows = slice(bi * TB, (bi + 1) * TB)
                blk = dict(kind="p", NT=TB, nseg=1, Ls=TB, first=(bi == 0), last=(bi == NPBLK - 1),
                           src=(xp if lfirst else x1p)[rows, :], dst=(yp if llast else x1p)[rows, :],
                           srck=("xp" if lfirst else "x1p%d" % bi), dstk=("yp%d" % bi if llast else "x1p%d" % bi))
                process_block(l, blk)
            if NPBLK:
                store_hist(l, 0, npool_p[l], nconv_p[l])
            if DO_SAMPLE:
                load_hist_sample(l)
                blk = dict(kind="s", NT=64, nseg=2, Ls=32,
                           src=(xs if lfirst else x1s), dst=(ys if llast else x1s),
                           srck=("xs" if lfirst else "x1s"), dstk=("ys" if llast else "x1s"))
                process_block(l, blk)
                for seg in range(2):
                    store_hist(l, seg, npool_s[l, seg], nconv_s[l, seg])
        P.emit()
    return nc


def _fm(v, ntile):
    sh = v.shape[:-1]
    return np.ascontiguousarray(np.moveaxis(v.reshape(sh + (ntile, 128)), -1, -2))


def make_in_maps(inp, n_cores=8):
    f = lambda a: np.ascontiguousarray(np.asarray(a, dtype=np.float32))
    x_prompt, x_sample = f(inp["x_prompt"]), f(inp["x_sample"])
    c_prompt, c_sample = f(inp["c_prompt"]), f(inp["c_sample"])
    state_pool, state_conv, state_ssm = f(inp["state_pool"]), f(inp["state_conv"]), f(inp["state_ssm"])
    w_ada, w_in, w_pool, w_out = f(inp["w_ada"]), f(inp["w_in"]), f(inp["w_pool"]), f(inp["w_out"])
    shared = {
        "w_ada": w_ada, "w_in": w_in, "w_pool": w_pool, "w_out": w_out,
        "badaT": _fm(f(inp["b_ada"]), 96),
        "pscT": _fm(f(inp["pool_scale"]), 16),
        "cwT": np.ascontiguousarray(np.transpose(f(inp["conv_w"]).reshape(2, 4, 64, 128), (0, 3, 2, 1))),
        "cbT": _fm(f(inp["conv_b"]), 64),
        "nwT": _fm(f(inp["ssd_norm_w"]), 48),
        "h96": np.ascontiguousarray(np.broadcast_to(
            np.stack([f(inp["dt_bias"]), f(inp["a_log"]), f(inp["d_skip"])], axis=1)[:, :, None, :], (2, 3, 128, 96))),
        "lng_bc": np.ascontiguousarray(np.broadcast_to(f(inp["ln_g"])[:, None, :], (2, 128, D))),
        "lnb_bc": np.ascontiguousarray(np.broadcast_to(f(inp["ln_b"])[:, None, :], (2, 128, D))),
        "c_ident": np.eye(128, dtype=np.float32),
        "c_ustrict": np.ascontiguousarray((np.arange(64)[:, None] > np.arange(64)[None, :]).astype(np.float32)),
        "c_tle": np.ascontiguousarray((np.arange(64)[:, None] <= np.arange(64)[None, :]).astype(np.float32)),
        "c_invc": np.ascontiguousarray(np.broadcast_to(
            np.stack([1.0 / np.minimum(np.arange(16) + 1, w) for w in (2, 4, 8, 16)]).astype(np.float32)[None], (128, 4, 16))),
    }
    maps = []
    for c in range(n_cores):
        pc = c % 4
        crow = np.stack([c_prompt[pc], c_sample[2 * c], c_sample[2 * c + 1]], axis=0)
        m = dict(shared)
        m["xp"] = x_prompt[pc]
        m["xs"] = np.ascontiguousarray(x_sample[2 * c:2 * c + 2].reshape(64, D))
        m["cT"] = np.ascontiguousarray(np.transpose(crow.reshape(3, NKT, 128), (2, 1, 0)))
        m["spool"] = np.ascontiguousarray(state_pool[:, 2 * c:2 * c + 2])
        m["sconv"] = np.ascontiguousarray(state_conv[:, 2 * c:2 * c + 2])
        m["sssm"] = np.ascontiguousarray(state_ssm[:, 2 * c:2 * c + 2].reshape(2, 2, W_SSD, 128))
        maps.append(m)
    return maps


def assemble(results):
    yp = np.stack([results[c]["yp"] for c in range(4)], axis=0)
    ys = np.concatenate([results[c]["ys"].reshape(2, 32, D) for c in range(8)], axis=0)
    npp = np.stack([results[c]["npool_p"] for c in range(4)], axis=1)
    ncp = np.stack([results[c]["nconv_p"] for c in range(4)], axis=1)
    nsp = np.stack([results[c]["nssm_p"].reshape(2, 96, 64, 128) for c in range(4)], axis=1)
    nps = np.concatenate([results[c]["npool_s"] for c in range(8)], axis=1)
    ncs = np.concatenate([results[c]["nconv_s"] for c in range(8)], axis=1)
    nss = np.concatenate([results[c]["nssm_s"].reshape(2, 2, 96, 64, 128) for c in range(8)], axis=1)
    return tuple(np.ascontiguousarray(a, dtype=np.float32) for a in (yp, ys, npp, ncp, nsp, nps, ncs, nss))


def kernel(**inputs):
    nc = build()
    in_maps = make_in_maps(inputs)
    res = run_bass_kernel_spmd(nc, in_maps, core_ids=list(range(8)))
    return assemble(res.results)
```
